# Optimizing a Trainium2 kernel written in Bass

```python
import jax, jax.numpy as jnp
from jax import lax
import numpy as np

D_MODEL = 1024
BATCH = 4
SEQ = 8192
DEPTH = 1
DEC_BATCH = 128
DEC_SEQ = 4
PAST_LEN = 16384
PAGE_SIZE = 128

MIX_WIDTH = D_MODEL
LRU_WIDTH = MIX_WIDTH // 2
LRU_BLOCKS = 8
LRU_BLOCK_DIM = LRU_WIDTH // LRU_BLOCKS
CONV_WIDTH = 4
LRU_C = 8.0
N_HEADS = 8
N_KV_HEADS = 2
GROUP = N_HEADS // N_KV_HEADS
HEAD_DIM = (MIX_WIDTH - LRU_WIDTH) // N_HEADS
Q_DIM = N_HEADS * HEAD_DIM
KV_DIM = N_KV_HEADS * HEAD_DIM
IN_DIM = 2 * LRU_WIDTH + Q_DIM + 2 * KV_DIM
SPLITS = (LRU_WIDTH, 2 * LRU_WIDTH, 2 * LRU_WIDTH + Q_DIM, 2 * LRU_WIDTH + Q_DIM + KV_DIM)
WINDOW = 128
BLOCK_Q = 128
ROPE_THETA = 10000.0
ATTN_SCALE = HEAD_DIM ** -0.5
D_FF = ((8 * D_MODEL + 3 * 256 - 1) // (3 * 256)) * 256
ALPHA = (2.0 * DEPTH) ** 0.25
BETA = (8.0 * DEPTH) ** -0.25

kernel_name = 'hymba_rglru_swa_sink_deepnorm_step'


def layer_norm(x, g, b, eps=1e-5):
    xf = x.astype(jnp.float32)
    mu = jnp.mean(xf, -1, keepdims=True)
    var = jnp.mean(jnp.square(xf - mu), -1, keepdims=True)
    return ((xf - mu) * lax.rsqrt(var + eps) * g.astype(jnp.float32) + b.astype(jnp.float32)).astype(x.dtype)


def rms_norm(x, g, eps=1e-6):
    xf = x.astype(jnp.float32)
    return (xf * lax.rsqrt(jnp.mean(xf * xf, -1, keepdims=True) + eps) * g.astype(jnp.float32)).astype(x.dtype)


def rope(x, positions):
    half = HEAD_DIM // 2
    inv = ROPE_THETA ** (-jnp.arange(half, dtype=jnp.float32) / half)
    ang = positions.astype(jnp.float32)[:, None] * inv[None, :]
    cos = jnp.cos(ang)[None, :, None, :]
    sin = jnp.sin(ang)[None, :, None, :]
    xf = x.astype(jnp.float32)
    x1, x2 = xf[..., :half], xf[..., half:]
    return jnp.concatenate([x1 * cos - x2 * sin, x2 * cos + x1 * sin], -1).astype(x.dtype)


def causal_conv(x_ext, conv_w, conv_b):
    s = x_ext.shape[1] - (CONV_WIDTH - 1)
    out = conv_b
    for j in range(CONV_WIDTH):
        out = out + x_ext[:, j:j + s] * conv_w[j]
    return out


def rg_lru(x, h0, w_a, b_a, w_x, b_x, lru_lambda):
    b, s, c = x.shape
    xb = x.reshape(b, s, LRU_BLOCKS, LRU_BLOCK_DIM)
    r = jax.nn.sigmoid(jnp.einsum('bshi,hij->bshj', xb, w_a) + b_a).reshape(b, s, c)
    i = jax.nn.sigmoid(jnp.einsum('bshi,hij->bshj', xb, w_x) + b_x).reshape(b, s, c)
    log_a = -LRU_C * r.astype(jnp.float32) * jax.nn.softplus(-lru_lambda.astype(jnp.float32))
    a = jnp.exp(log_a)
    u = jnp.sqrt(-jnp.expm1(2.0 * log_a)) * (i * x).astype(jnp.float32)

    def step(h, au):
        a_t, u_t = au
        h = a_t * h + u_t
        return h, h

    h_last, hs = lax.scan(step, h0.astype(jnp.float32), (a.transpose(1, 0, 2), u.transpose(1, 0, 2)))
    return hs.transpose(1, 0, 2).astype(x.dtype), h_last


def sink_softmax(scores, mask, sinks):
    s = jnp.where(mask, scores.astype(jnp.float32), -jnp.inf)
    sink = sinks.astype(jnp.float32).reshape(N_KV_HEADS, GROUP, 1, 1)
    m = jnp.maximum(jnp.max(s, -1, keepdims=True), sink)
    p = jnp.exp(s - m)
    return p / (jnp.sum(p, -1, keepdims=True) + jnp.exp(sink - m))


def swa_banded(q, k, v, sinks):
    b, s = q.shape[:2]
    nb = s // BLOCK_Q
    qb = q.reshape(b, nb, BLOCK_Q, N_KV_HEADS, GROUP, HEAD_DIM)

    def with_prev(t):
        tb = t.reshape(b, nb, BLOCK_Q, N_KV_HEADS, HEAD_DIM)
        prev = jnp.pad(tb, ((0, 0), (1, 0), (0, 0), (0, 0), (0, 0)))[:, :-1]
        return jnp.concatenate([prev, tb], axis=2)

    kk, vv = with_prev(k), with_prev(v)
    blk = jnp.arange(nb)[:, None] * BLOCK_Q
    qabs = blk + jnp.arange(BLOCK_Q)[None, :]
    kabs = blk - BLOCK_Q + jnp.arange(2 * BLOCK_Q)[None, :]
    diff = qabs[:, :, None] - kabs[:, None, :]
    mask = (diff >= 0) & (diff <= WINDOW) & (kabs[:, None, :] >= 0)
    mask = mask[:, None, None]
    scores = jnp.einsum('bnqkgd,bnskd->bnkgqs', qb, kk) * ATTN_SCALE
    p = sink_softmax(scores, mask, sinks)
    out = jnp.einsum('bnkgqs,bnskd->bnqkgd', p, vv.astype(jnp.float32))
    return out.reshape(b, s, Q_DIM).astype(q.dtype)


def swa_with_buffer(q, k, v, k_buf, v_buf, sinks, positions):
    b, t = q.shape[:2]
    w = k_buf.shape[1]
    kk = jnp.concatenate([k_buf.astype(k.dtype), k], axis=1)
    vv = jnp.concatenate([v_buf.astype(v.dtype), v], axis=1)
    kpos = jnp.concatenate([positions[0] - w + jnp.arange(w), positions])
    diff = positions[:, None] - kpos[None, :]
    mask = (diff >= 0) & (diff <= WINDOW)
    qg = q.reshape(b, t, N_KV_HEADS, GROUP, HEAD_DIM)
    scores = jnp.einsum('btkgd,bskd->bkgts', qg, kk) * ATTN_SCALE
    p = sink_softmax(scores, mask, sinks)
    out = jnp.einsum('bkgts,bskd->btkgd', p, vv.astype(jnp.float32)).reshape(b, t, Q_DIM).astype(q.dtype)
    return out, kk[:, -w:], vv[:, -w:]


def decoder_layer(x, positions, conv_hist, lru_h0, k_buf, v_buf,
                  w_in, b_in, conv_w, conv_b, w_a, b_a, w_x, b_x, lru_lambda, sinks,
                  g_lru, g_attn, w_out, b_out, ln1_g, ln1_b, w_gate, w_up, w_down, ln2_g, ln2_b):
    bsz, seq = x.shape[:2]
    z = jnp.einsum('bsd,de->bse', x, w_in) + b_in
    xr, gate, q, k, v = jnp.split(z, SPLITS, axis=-1)
    x_ext = jnp.concatenate([conv_hist.astype(xr.dtype), xr], axis=1)
    xc = causal_conv(x_ext, conv_w, conv_b)
    lru_out, lru_last = rg_lru(xc, lru_h0, w_a, b_a, w_x, b_x, lru_lambda)
    y_lru = jax.nn.gelu(gate) * lru_out
    new_conv = x_ext[:, -(CONV_WIDTH - 1):]
    q = rope(q.reshape(bsz, seq, N_HEADS, HEAD_DIM), positions)
    k = rope(k.reshape(bsz, seq, N_KV_HEADS, HEAD_DIM), positions)
    v = v.reshape(bsz, seq, N_KV_HEADS, HEAD_DIM)
    if k_buf is None:
        y_attn = swa_banded(q, k, v, sinks)
        keep = min(WINDOW, seq)
        new_k, new_v = k[:, -keep:], v[:, -keep:]
    else:
        y_attn, new_k, new_v = swa_with_buffer(q, k, v, k_buf, v_buf, sinks, positions)
    mixed = jnp.concatenate([rms_norm(y_lru, g_lru), rms_norm(y_attn, g_attn)], axis=-1)
    y_mix = jnp.einsum('bse,ed->bsd', mixed, w_out) + b_out
    h1 = layer_norm(ALPHA * x + y_mix, ln1_g, ln1_b)
    hid = jax.nn.silu(jnp.einsum('bsd,df->bsf', h1, w_gate)) * jnp.einsum('bsd,df->bsf', h1, w_up)
    ffn = jnp.einsum('bsf,fd->bsd', hid, w_down)
    h2 = layer_norm(ALPHA * h1 + ffn, ln2_g, ln2_b)
    return h2, new_conv, lru_last, new_k, new_v


def setup_inputs(seed: int = 0) -> dict:
    key = jax.random.key(seed)
    ks = jax.random.split(key, 32)
    nrm = jax.random.normal
    win = min(WINDOW, PAST_LEN)
    x_prompt = nrm(ks[0], (BATCH, SEQ, D_MODEL), jnp.float32)
    x_sample = nrm(ks[1], (DEC_BATCH, DEC_SEQ, D_MODEL), jnp.float32)
    cache_k_win = nrm(ks[2], (DEPTH, DEC_BATCH, win, N_KV_HEADS, HEAD_DIM), jnp.float32)
    cache_v_win = nrm(ks[3], (DEPTH, DEC_BATCH, win, N_KV_HEADS, HEAD_DIM), jnp.float32)
    state_conv = nrm(ks[4], (DEPTH, DEC_BATCH, CONV_WIDTH - 1, LRU_WIDTH), jnp.float32)
    state_lru = 0.5 * nrm(ks[5], (DEPTH, DEC_BATCH, LRU_WIDTH), jnp.float32)
    w_in = nrm(ks[6], (DEPTH, D_MODEL, IN_DIM), jnp.float32) * D_MODEL ** -0.5
    v_scale = jnp.concatenate([jnp.ones((IN_DIM - KV_DIM,), jnp.float32), jnp.full((KV_DIM,), BETA, jnp.float32)])
    w_in = w_in * v_scale
    b_in = 0.01 * nrm(ks[7], (DEPTH, IN_DIM), jnp.float32)
    conv_w = nrm(ks[8], (DEPTH, CONV_WIDTH, LRU_WIDTH), jnp.float32) * CONV_WIDTH ** -0.5
    conv_b = 0.01 * nrm(ks[9], (DEPTH, LRU_WIDTH), jnp.float32)
    w_a = nrm(ks[10], (DEPTH, LRU_BLOCKS, LRU_BLOCK_DIM, LRU_BLOCK_DIM), jnp.float32) * LRU_BLOCK_DIM ** -0.5
    b_a = 0.01 * nrm(ks[11], (DEPTH, LRU_BLOCKS, LRU_BLOCK_DIM), jnp.float32)
    w_x = nrm(ks[12], (DEPTH, LRU_BLOCKS, LRU_BLOCK_DIM, LRU_BLOCK_DIM), jnp.float32) * LRU_BLOCK_DIM ** -0.5
    b_x = 0.01 * nrm(ks[13], (DEPTH, LRU_BLOCKS, LRU_BLOCK_DIM), jnp.float32)
    a_init = jax.random.uniform(ks[14], (DEPTH, LRU_WIDTH), jnp.float32, minval=0.9, maxval=0.999)
    sig = a_init ** (1.0 / LRU_C)
    lru_lambda = jnp.log(sig) - jnp.log1p(-sig)
    sinks = 0.5 * nrm(ks[15], (DEPTH, N_HEADS), jnp.float32)
    g_lru = 1.0 + 0.02 * nrm(ks[16], (DEPTH, LRU_WIDTH), jnp.float32)
    g_attn = 1.0 + 0.02 * nrm(ks[17], (DEPTH, Q_DIM), jnp.float32)
    w_out = nrm(ks[18], (DEPTH, MIX_WIDTH, D_MODEL), jnp.float32) * MIX_WIDTH ** -0.5 * BETA
    b_out = 0.01 * nrm(ks[19], (DEPTH, D_MODEL), jnp.float32)
    ln1_g = 1.0 + 0.02 * nrm(ks[20], (DEPTH, D_MODEL), jnp.float32)
    ln1_b = 0.01 * nrm(ks[21], (DEPTH, D_MODEL), jnp.float32)
    w_gate = nrm(ks[22], (DEPTH, D_MODEL, D_FF), jnp.float32) * D_MODEL ** -0.5 * BETA
    w_up = nrm(ks[23], (DEPTH, D_MODEL, D_FF), jnp.float32) * D_MODEL ** -0.5 * BETA
    w_down = nrm(ks[24], (DEPTH, D_FF, D_MODEL), jnp.float32) * D_FF ** -0.5 * BETA
    ln2_g = 1.0 + 0.02 * nrm(ks[25], (DEPTH, D_MODEL), jnp.float32)
    ln2_b = 0.01 * nrm(ks[26], (DEPTH, D_MODEL), jnp.float32)
    return {'x_prompt': x_prompt, 'x_sample': x_sample,
            'cache_k_win': cache_k_win, 'cache_v_win': cache_v_win,
            'state_conv': state_conv, 'state_lru': state_lru,
            'w_in': w_in, 'b_in': b_in, 'conv_w': conv_w, 'conv_b': conv_b,
            'w_a': w_a, 'b_a': b_a, 'w_x': w_x, 'b_x': b_x, 'lru_lambda': lru_lambda,
            'sinks': sinks, 'g_lru': g_lru, 'g_attn': g_attn, 'w_out': w_out, 'b_out': b_out,
            'ln1_g': ln1_g, 'ln1_b': ln1_b, 'w_gate': w_gate, 'w_up': w_up, 'w_down': w_down,
            'ln2_g': ln2_g, 'ln2_b': ln2_b}


def reference(x_prompt, x_sample, cache_k_win, cache_v_win, state_conv, state_lru,
              w_in, b_in, conv_w, conv_b, w_a, b_a, w_x, b_x, lru_lambda,
              sinks, g_lru, g_attn, w_out, b_out, ln1_g, ln1_b, w_gate, w_up, w_down, ln2_g, ln2_b):
    pos_p = jnp.arange(x_prompt.shape[1], dtype=jnp.int32)
    pos_s = PAST_LEN + jnp.arange(x_sample.shape[1], dtype=jnp.int32)
    bp = x_prompt.shape[0]
    hp, hs = x_prompt, x_sample
    conv_p, lru_p, kp_l, vp_l = [], [], [], []
    conv_s, lru_s, ks_l, vs_l = [], [], [], []
    for l in range(DEPTH):
        p = (w_in[l], b_in[l], conv_w[l], conv_b[l], w_a[l], b_a[l], w_x[l], b_x[l], lru_lambda[l], sinks[l],
             g_lru[l], g_attn[l], w_out[l], b_out[l], ln1_g[l], ln1_b[l], w_gate[l], w_up[l], w_down[l],
             ln2_g[l], ln2_b[l])
        zero_conv = jnp.zeros((bp, CONV_WIDTH - 1, LRU_WIDTH), x_prompt.dtype)
        zero_h = jnp.zeros((bp, LRU_WIDTH), jnp.float32)
        hp, c1, h1, k1, v1 = decoder_layer(hp, pos_p, zero_conv, zero_h, None, None, *p)
        hs, c2, h2, k2, v2 = decoder_layer(hs, pos_s, state_conv[l], state_lru[l], cache_k_win[l], cache_v_win[l], *p)
        conv_p.append(c1); lru_p.append(h1); kp_l.append(k1); vp_l.append(v1)
        conv_s.append(c2); lru_s.append(h2); ks_l.append(k2); vs_l.append(v2)
    return (hp, hs,
            jnp.stack(conv_p), jnp.stack(lru_p), jnp.stack(kp_l), jnp.stack(vp_l),
            jnp.stack(conv_s), jnp.stack(lru_s), jnp.stack(ks_l), jnp.stack(vs_l))
```

```python
import numpy as np
from contextlib import ExitStack
import concourse.bass as bass
import concourse.mybir as mybir
from concourse.bass_utils import run_bass_kernel_spmd

F32 = mybir.dt.float32
BF16 = mybir.dt.bfloat16
AF = mybir.ActivationFunctionType
ALU = mybir.AluOpType

NCORES = 8
D = 1024
SEQ = 8192
HALF = 4096
NB = 16
NS = 64
LRUW = 512
DFF = 2816
NF = 22
ALPHA = float(2.0 ** 0.25)
SCALE = 0.125
EPS_LN = 1e-5
EPS_RMS = 1e-6
PAST = 16384
NV = 768 + 512 + 5 * 1024 + 8
OFF_BQKV, OFF_GATT, OFF_BOUT, OFF_L1G, OFF_L1B, OFF_L2G, OFF_L2B, OFF_SINK = (
    0, 768, 1280, 2304, 3328, 4352, 5376, 6400)
C_BXR, C_BGATE, C_CONVW, C_CONVB, C_BA, C_BX, C_LAM, C_GLRU = 0, 4, 8, 24, 28, 32, 36, 40
NBF = 44


class Buf:
    __slots__ = ("name", "w", "r")

    def __init__(self, name):
        self.name = name
        self.w = None
        self.r = {}


class Prog:
    ENG = ("pe", "act", "dve", "pool", "sp")

    def __init__(self, nc, es):
        self.nc = nc
        self.es = es
        self.q = {e: [] for e in self.ENG}
        self.sem = {}
        self.cnt = {}
        self.known = {e: {} for e in self.ENG}
        for e in self.ENG:
            self.sem[e] = es.enter_context(nc.semaphore("s_" + e))
            self.cnt[e] = 0
        self.nbufs = 0
        self.rec = None

    def record(self, fn):
        assert self.rec is None
        self.rec = []
        try:
            fn()
            return self.rec
        finally:
            self.rec = None

    DUR = {"pe": 0.33, "act": 0.68, "dve": 0.65, "pool": 1.25, "sp": 0.1}

    def merge(self, lists, skew=False, offs=None):
        n = len(lists)
        idx = [0] * n
        tfree = {e: 0.0 for e in self.ENG}
        ready = {}
        lastrd = {}
        tset = [None]
        LAT = 0.12
        remaining = sum(len(l) for l in lists)

        def est(it):
            if it[0] == "op":
                eng, reads, writes, meta = it[1], it[3], it[4], it[5]
            else:
                eng, reads, writes, meta = it[1], it[4], it[5], None
            t = tfree[eng]
            for b_ in reads:
                t = max(t, ready.get(id(b_), 0.0) + LAT)
            for b_ in writes:
                t = max(t, ready.get(id(b_), 0.0) + LAT, lastrd.get(id(b_), 0.0) + LAT)
            dur = self.DUR[eng] if it[0] == "op" else 2.0
            kind = None
            if meta is not None:
                kind, dur = meta
            pen = 0.0
            if kind in ("te", "sq") and tset[0] not in (None, kind):
                pen = 2.6
            return t + pen, dur + pen, eng, reads, writes, kind

        while remaining > 0:
            best = None
            for i, l in enumerate(lists):
                if idx[i] < len(l):
                    e_ = est(l[idx[i]])
                    key = (e_[0], i)
                    if best is None or key < best[0]:
                        best = (key, i, e_)
            _, i, (t0, dur, eng, reads, writes, meta) = best
            it = lists[i][idx[i]]
            idx[i] += 1
            remaining -= 1
            t1 = t0 + dur
            if it[0] == "op":
                tfree[eng] = t1
                if meta in ("te", "sq"):
                    tset[0] = meta
                self.op(it[1], it[2], it[3], it[4])
            else:
                tfree[eng] = t0 + 0.1
                self.dma(it[1], it[2], it[3], it[4], it[5], it[6])
            for b_ in reads:
                lastrd[id(b_)] = max(lastrd.get(id(b_), 0.0), t1)
            for b_ in writes:
                ready[id(b_)] = t1

    def buf(self, name=None):
        self.nbufs += 1
        return Buf(name or ("b%d" % self.nbufs))

    def bufs(self, n, name="b"):
        return [self.buf("%s%d" % (name, i)) for i in range(n)]

    def _wait(self, eng, k, v):
        if k == eng and eng in ("pe", "sp"):
            return
        if self.known[eng].get(k, 0) < v:
            self.known[eng][k] = v
            self.q[eng].append(("wait", k, v))

    def _deps(self, eng, reads, writes):
        deps = {}

        def add(k, v):
            if deps.get(k, 0) < v:
                deps[k] = v

        for b in reads:
            if b.w is not None:
                add(*b.w)
        for b in writes:
            if b.w is not None:
                add(*b.w)
            for k, v in b.r.items():
                add(k, v)
        for k, v in deps.items():
            self._wait(eng, k, v)

    def _mark(self, t, reads, writes):
        for b in reads:
            if b.r.get(t[0], 0) < t[1]:
                b.r[t[0]] = t[1]
        for b in writes:
            b.w = t
            b.r = {}

    def op(self, eng, fn, reads=(), writes=(), meta=None):
        if self.rec is not None:
            self.rec.append(("op", eng, fn, list(reads), list(writes), meta))
            return None
        self._deps(eng, reads, writes)
        self.cnt[eng] += 1
        t = (eng, self.cnt[eng])
        self.q[eng].append(("op", fn))
        self._mark(t, reads, writes)
        return t

    def dma(self, qeng, out, in_, reads=(), writes=(), chan=None):
        if self.rec is not None:
            self.rec.append(("dma", qeng, out, in_, list(reads), list(writes), chan))
            return None
        if chan not in self.sem:
            self.sem[chan] = self.es.enter_context(self.nc.semaphore("d_" + chan))
            self.cnt[chan] = 0
        self._deps(qeng, reads, writes)
        if self.cnt[chan] > 0:
            self._wait(qeng, chan, self.cnt[chan])
        self.cnt[chan] += 16
        t = (chan, self.cnt[chan])
        self.q[qeng].append(("dma", out, in_, chan))
        self._mark(t, reads, writes)
        return t

    def fence(self, bufs, engines=("pe", "act", "dve")):
        for eng in engines:
            self._deps(eng, (), bufs)

    @staticmethod
    def _n(ap):
        try:
            v = ap.free_size
            return int(v() if callable(v) else v)
        except Exception:
            return 512

    def mm(self, out, lhsT, rhs, start, stop, reads, writes):
        d = 0.06 + self._n(out) * 0.00075 * (4 if lhsT.dtype == F32 else 1)
        return self.op("pe", lambda e: e.matmul(out, lhsT, rhs, start=start, stop=stop), reads, writes, meta=(None, d))

    def tr(self, out, in_, ident, reads, writes):
        return self.op("pe", lambda e: e.transpose(out, in_, ident), reads, writes, meta=(None, 0.2))

    def act(self, out, in_, func, reads, writes, bias=None, scale=None, accum_out=None):
        kw = {}
        if bias is not None:
            kw["bias"] = bias
        if scale is not None:
            kw["scale"] = scale
        if accum_out is not None:
            kw["accum_out"] = accum_out
        kind = "sq" if func == AF.Sqrt else ("te" if func in (AF.Tanh, AF.Exp) else None)
        return self.op("act", lambda e: e.activation(out, in_, func, **kw), reads, writes,
                       meta=(kind, 0.22 + self._n(out) * 0.00075))

    def _dur(self, eng, out, two_src):
        n = self._n(out)
        if eng == "pool":
            return 0.2 + n * 0.0021
        if eng == "act":
            return 0.22 + n * 0.00075
        return 0.07 + n * (0.0021 if two_src else 0.00105)

    def cp(self, eng, out, in_, reads, writes):
        m = (None, self._dur(eng, out, False))
        if eng == "act":
            return self.op("act", lambda e: e.copy(out, in_), reads, writes, meta=m)
        return self.op(eng, lambda e: e.tensor_copy(out, in_), reads, writes, meta=m)

    def tt(self, eng, out, in0, in1, op, reads, writes):
        two = not (str(in0.space).upper().find("PSUM") >= 0 or str(in1.space).upper().find("PSUM") >= 0)
        return self.op(eng, lambda e: e.tensor_tensor(out, in0, in1, op), reads, writes, meta=(None, self._dur(eng, out, two)))

    def ts(self, eng, out, in0, s1, s2, op0, op1, reads, writes):
        m = (None, self._dur(eng, out, False))
        if op1 is None:
            return self.op(eng, lambda e: e.tensor_scalar(out, in0, s1, None, op0), reads, writes, meta=m)
        return self.op(eng, lambda e: e.tensor_scalar(out, in0, s1, s2, op0, op1), reads, writes, meta=m)

    def stt(self, out, in0, scalar, in1, op0, op1, reads, writes, accum_out=None):
        two = not (str(in0.space).upper().find("PSUM") >= 0 or str(in1.space).upper().find("PSUM") >= 0)
        m = (None, self._dur("dve", out, two))
        if accum_out is not None:
            return self.op("dve", lambda e: e.scalar_tensor_tensor(out, in0, scalar, in1, op0, op1, accum_out=accum_out),
                           reads, writes, meta=m)
        return self.op("dve", lambda e: e.scalar_tensor_tensor(out, in0, scalar, in1, op0, op1), reads, writes, meta=m)

    def memset(self, eng, ap, val, writes):
        return self.op(eng, lambda e: e.memset(ap, val), (), writes)

    def emit(self):
        nc = self.nc
        for k, v in self.cnt.items():
            if k not in self.ENG and v > 0:
                self._wait("sp", k, v)
        block = self.es.enter_context(nc.Block())
        sem = self.sem

        def replay(name, e):
            own = sem[name]
            for it in self.q[name]:
                if it[0] == "wait":
                    e.wait_ge(sem[it[1]], it[2])
                elif it[0] == "op":
                    it[1](e).then_inc(own, 1)
                else:
                    e.dma_start(out=it[1], in_=it[2]).then_inc(sem[it[3]], 16)

        @block.sync
        def _(e):
            replay("sp", e)

        @block.gpsimd
        def _(e):
            replay("pool", e)

        @block.tensor
        def _(e):
            replay("pe", e)

        @block.scalar
        def _(e):
            replay("act", e)

        @block.vector
        def _(e):
            replay("dve", e)


def build_program(n_pre=8, n_main=8, do_sample=True, dbg=False, stage=99, PREFIX_OFFS=(0, 0, 0.5, 0.5)):
    nc = bass.Bass("TRN2", target_bir_lowering=False)
    es = ExitStack()

    def din(name, shape, dt=F32):
        return nc.dram_tensor(name, list(shape), dt, kind="ExternalInput").ap()

    def dout(name, shape, dt=F32):
        return nc.dram_tensor(name, list(shape), dt, kind="ExternalOutput").ap()

    xpre = din("xpre", [HALF, D])
    xmain = din("xmain", [HALF, D])
    xs = din("xs", [NS, D])
    ck = din("ck", [NB, 128, 128])
    cv = din("cv", [NB, 128, 128])
    sconv = din("sconv", [48, LRUW])
    slru = din("slru", [NB, LRUW])
    w_in = din("w_in", [D, 1792])
    w_out = din("w_out", [D, D])
    w_gate = din("w_gate", [D, DFF])
    w_up = din("w_up", [D, DFF])
    w_down = din("w_down", [DFF, D])
    wab_d = din("wab", [128, 4, 128])
    wxb_d = din("wxb", [128, 4, 128])
    bfm_d = din("bfm", [128, NBF])
    vrow_d = din("vrow", [1, NV])
    ropem_d = din("ropem", [8, 128, 4, 96])
    ropex_d = din("ropex", [128, 96])
    ropes_d = din("ropes", [NS, 96])
    maskp_d = din("maskp", [128, 2, 128])
    masks_d = din("masks", [128, 17, 64])
    ident_d = din("ident", [128, 128])
    flag_d = din("flag", [128, 1])

    yp = dout("yp", [HALF, D])
    ys = dout("ys", [NS, D])
    convp = dout("convp", [3, LRUW])
    lrup = dout("lrup", [1, LRUW])
    kp = dout("kp", [128, 128])
    vp = dout("vp", [128, 128])
    convs = dout("convs", [48, LRUW])
    lrus = dout("lrus", [NB, LRUW])
    ks = dout("ks", [NB, 128, 128])
    vs = dout("vs", [NB, 128, 128])

    wg_b = nc.dram_tensor("wg_b", [D, DFF], BF16, kind="Internal").ap()
    wu_b = nc.dram_tensor("wu_b", [D, DFF], BF16, kind="Internal").ap()
    wd_b = nc.dram_tensor("wd_b", [DFF, D], BF16, kind="Internal").ap()

    p = Prog(nc, es)

    class Cut(Exception):
        pass

    cks = []

    def ckp(name):
        cks.append(name)
        if dbg and len(cks) == dbg:
            print("CUT at checkpoint", len(cks), name)
            raise Cut()

    def sb(name, shape, dt=F32):
        return es.enter_context(nc.sbuf_tensor(name, list(shape), dt))

    w_in_sb = sb("w_in_sb", [128, 8, 1792], BF16)
    w_out_sb = sb("w_out_sb", [128, 8, 1024], BF16)
    wab_sb = sb("wab_sb", [128, 4, 128], BF16)
    wxb_sb = sb("wxb_sb", [128, 4, 128], BF16)
    vrow_sb = sb("vrow_sb", [128, NV])
    bfm_sb = sb("bfm_sb", [128, NBF])
    sc_sb = sb("sc_sb", [128, 8])
    hb_sb = sb("hb_sb", [128, 8])
    esink_sb = sb("esink_sb", [128, 8])
    ident_f = sb("ident_f", [128, 128])
    ident_b = sb("ident_b", [128, 128], BF16)
    ones_f = sb("ones_f", [128, 8])
    zero_b = sb("zero_b", [128, 272], BF16)
    zero_f = zero_b[:, 0:256].bitcast(F32)
    maskp_sb = sb("maskp_sb", [128, 2, 128], BF16)
    masks_sb = sb("masks_sb", [128, 17, 64], BF16)
    flag_sb = sb("flag_sb", [128, 1])
    xtok = sb("xtok", [128, 4, 1024])
    rope_sb = sb("rope_sb", [128, 4, 96])
    h1 = sb("h1", [128, 4, 1024])
    mixT = sb("mixT", [128, 8, 512], BF16)
    kT = sb("kT", [128, 4, 640], BF16)
    vaug = sb("vaug", [128, 5, 2, 65], BF16)
    hist = sb("hist", [128, 4, 48])
    hcar = sb("hcar", [128, 4, 16])
    wgu = sb("wgu", [128, 3, 2, 8, 128], BF16)
    wdn = sb("wdn", [128, 3, 2, 512], BF16)
    hidT = sb("hidT", [128, NF, 512], BF16)
    h1T = sb("h1T", [128, 8, 512], BF16)
    xr_ext = sb("xr_ext", [128, 2, 516])
    gg = sb("gg", [128, 2, 512])
    hh = sb("hh", [128, 2, 512])
    NTMP = 4
    tmp = sb("tmp", [128, NTMP, 1024])
    qrot = sb("qrot", [128, 512], BF16)
    kz = sb("kz", [128, 512], BF16)
    yn = sb("yn", [128, 2, 512], BF16)
    xcb = sb("xcb", [128, 2, 512], BF16)
    st = sb("st", [128, 192])
    bnst = sb("bnst", [128, 4, 2, 6])
    k32 = sb("k32", [128, 128])

    psf = [es.enter_context(nc.psum_tensor("psf%d" % i, [128, 512], F32)) for i in range(6)]
    psbs = [es.enter_context(nc.psum_tensor("psb%d" % i, [128, 1024], BF16)) for i in range(2)]
    B_psf = p.bufs(6, "psf")
    B_psb = p.bufs(2, "psb")
    ps_rr = [0]
    psb_rr = [0]

    def psum():
        i = ps_rr[0] % 4
        ps_rr[0] += 1
        return psf[i], B_psf[i]

    ps_rr6 = [0]

    def psum6():
        i = ps_rr6[0] % 6
        ps_rr6[0] += 1
        return psf[i], B_psf[i]

    def psumb():
        i = psb_rr[0] % 2
        psb_rr[0] += 1
        return psbs[i][:, 0:512], B_psb[i]

    tmp_rr = [0]
    B_tmp = p.bufs(NTMP, "tmp")

    class ChainRes:
        def __init__(self, par, banks=None, tmps=None, psb=None, idx=None):
            self.par = par
            self.idx = par if idx is None else idx
            self.banks = banks if banks is not None else [2 * par, 2 * par + 1]
            self.n = 0
            self.sto = 40 * self.idx
            self.tmps = tmps if tmps is not None else [(tmp[:, 2 * par + i, :], B_tmp[2 * par + i]) for i in range(2)]
            self.psb = psb if psb is not None else [par, par]

        def psum(self):
            i = self.banks[self.n % len(self.banks)]
            self.n += 1
            return psf[i], B_psf[i]

        def tmp(self, k):
            return self.tmps[k]

        def psumb(self, k):
            i = self.psb[k]
            return psbs[i][:, 0:512], B_psb[i]

        slots = None

        def slot(self, name):
            if self.slots is not None:
                return self.slots[name]
            return {"xe": (xr_ext[:, self.par, :], B_xr[self.par]), "gg": (gg[:, self.par, :], B_gg[self.par]),
                    "hh": (hh[:, self.par, :], B_hh[self.par]), "xcb": (xcb[:, self.par, :], B_xcb[self.par])}[name]

    CH = [ChainRes(0), ChainRes(1)]
    B_tq = p.bufs(2, "tq")
    tq = h1T[:, :, :].rearrange("p a b -> p (a b)").bitcast(F32).rearrange("p (i x) -> p i x", i=2)
    CH_LA = ChainRes(0, banks=[0])
    CH_LB = ChainRes(1, banks=[1])
    CH_Q = ChainRes(0, banks=[2, 3], tmps=[(tq[:, 0, :], B_tq[0]), (tq[:, 1, :], B_tq[1])], psb=[0, 1])
    CH_S = ChainRes(0, psb=[0, 1])
    CH4 = [ChainRes(i % 2, banks=[i], idx=i) for i in range(4)]

    def gettmp():
        i = tmp_rr[0] % NTMP
        tmp_rr[0] += 1
        return tmp[:, i, :], B_tmp[i]

    B_win = p.bufs(8, "win")
    B_wout = p.bufs(2, "wout")
    B_const = p.buf("const")
    B_xtok = p.bufs(4, "xtok")
    B_rope = p.buf("rope")
    B_h1 = p.bufs(4, "h1")
    B_mixT = [[p.buf() for _ in range(4)] for _ in range(8)]
    B_kT = p.bufs(5, "kT")
    B_vaug = p.bufs(5, "vaug")
    B_hist = p.bufs(4, "hist")
    B_hcar = p.bufs(4, "hcar")
    B_wg = p.bufs(3, "wg")
    B_wu = p.bufs(3, "wu")
    B_wdn = p.bufs(3, "wdn")
    B_hid = p.bufs(NF, "hid")
    B_h1T = p.bufs(4, "h1T")
    B_xT = p.bufs(4, "xT")
    B_qT = p.bufs(4, "qT")
    B_pT = p.bufs(4, "pT")
    B_kTc = p.bufs(2, "kTc")
    B_vaugc = p.buf("vaugc")
    B_kcd = p.buf("kcd")
    B_xr = p.bufs(2, "xr")
    B_gg = p.bufs(2, "gg")
    B_hh = p.bufs(2, "hh")
    B_xcb = p.bufs(2, "xcb")
    B_qrot = p.buf("qrot")
    B_kdup = p.buf("kdup")
    B_yn = p.bufs(2, "yn")
    B_st2 = p.bufs(4, "st")
    B_st = B_st2[0]
    B_bnst = p.bufs(4, "bnst")
    B_k32 = p.buf("k32")
    B_wgb = p.buf("wgb")
    B_wub = p.buf("wub")
    B_wdb = p.buf("wdb")
    B_rstd = p.buf("rstd")
    B_rstda = p.bufs(4, "rstda")

    xT = hidT[:, 0:8, :]
    qT = hidT[:, 8:12, :]
    pT_all = hidT[:, 12:20, :]
    ARENA_A = B_xT + B_qT + B_pT + B_kTc + [B_kcd]
    ARENA_B = B_hid + B_h1T

    rstd_sb = sb("rstd_sb", [128, 16])

    B_px = p.bufs(12, "px")
    hq = hidT[:, 8:16, :].rearrange("p a b -> p (a b)").bitcast(F32).rearrange("p (i x) -> p i x", i=2)
    xe2 = hidT[:, 16:21, :].rearrange("p a b -> p (a b)").bitcast(F32)[:, 0:1032].rearrange("p (i x) -> p i x", i=2)
    CHP = []
    _ptmps = [[(tmp[:, 0, :], B_tmp[0]), (tmp[:, 1, :], B_tmp[1])], [(tmp[:, 2, :], B_tmp[2]), (tmp[:, 3, :], B_tmp[3])],
              [(tq[:, 0, :], B_tq[0]), (tq[:, 1, :], B_tq[1])], [(hq[:, 0, :], B_px[0]), (hq[:, 1, :], B_px[1])]]
    for i in range(4):
        r = ChainRes(i % 2, banks=[i], tmps=_ptmps[i], idx=i)
        if i >= 2:
            r.slots = {"xe": (xe2[:, i - 2, :], B_px[2 + i]), "gg": (None, None),
                       "hh": (gg[:, i - 2, :], B_gg[i - 2]), "xcb": (mixT[:, i - 2, :], B_px[6 + i])}
        CHP.append(r)
    PREFIX_ALIAS = B_px + B_tq

    B_c = {n: p.buf("c_" + n) for n in ("ident", "bfm", "vrow", "flag", "identb", "wab", "wxb", "maskp", "masks", "misc", "sc")}
    p.dma("sp", ident_f[:], ident_d, writes=[B_c["ident"]], chan="c0")
    p.dma("sp", bfm_sb[:], bfm_d, writes=[B_c["bfm"]], chan="c1")
    B_winA = p.bufs(8, "winA")
    for k in range(8):
        p.dma("pool", w_in_sb[:, k, 0:512], w_in[k * 128:(k + 1) * 128, 0:512], writes=[B_winA[k]], chan="g%d" % (1 + k % 4))
    for k in range(8):
        p.dma("pool", w_in_sb[:, k, 512:1792], w_in[k * 128:(k + 1) * 128, 512:1792], writes=[B_win[k]], chan="g%d" % (1 + k % 4))
    p.dma("pool", ident_b[:], ident_d, writes=[B_c["identb"]], chan="g0")
    p.dma("pool", wab_sb[:], wab_d, writes=[B_c["wab"]], chan="g5")
    p.dma("pool", wxb_sb[:], wxb_d, writes=[B_c["wxb"]], chan="g6")
    p.dma("sp", vrow_sb[:], vrow_d[0].partition_broadcast(128), writes=[B_c["vrow"]], chan="c2")
    p.dma("sp", flag_sb[:], flag_d, writes=[B_c["flag"]], chan="c3")
    p.dma("pool", maskp_sb[:], maskp_d, writes=[B_c["maskp"]], chan="g7")
    p.dma("pool", masks_sb[:], masks_d, writes=[B_c["masks"]], chan="g8")
    for hlf in range(2):
        p.dma("pool", w_out_sb[:, 4 * hlf:4 * hlf + 4, :],
              w_out[hlf * 512:(hlf + 1) * 512, :].rearrange("(k p) n -> p k n", p=128),
              writes=[B_wout[hlf]], chan="g%d" % (9 + hlf))
    p.dma("pool", wg_b, w_gate, reads=B_win, writes=[B_wgb], chan="g11")
    p.dma("pool", wu_b, w_up, reads=B_win, writes=[B_wub], chan="g12")
    p.dma("pool", wd_b, w_down, reads=B_win, writes=[B_wdb], chan="g13")

    p.memset("dve", ones_f[:], 1.0, [B_c["misc"]])
    p.memset("dve", zero_b[:], 0.0, [B_c["misc"]])
    p.memset("dve", vaug[:, :, :, 64:65], 1.0, B_vaug)
    p.memset("dve", hist[:], 0.0, B_hist)
    p.memset("dve", hcar[:], 0.0, B_hcar)
    p.memset("dve", kT[:], 0.0, B_kT)
    p.memset("dve", kz[:], 0.0, [B_kdup])
    p.act(st[:, 0:4], bfm_sb[:, C_LAM:C_LAM + 4], AF.Exp, [B_c["bfm"]], [B_st], scale=-1.0)
    p.act(st[:, 4:8], st[:, 0:4], AF.Ln, [B_st], [B_st], bias=1.0)
    p.ts("dve", sc_sb[:, 0:4], st[:, 4:8], -8.0, None, ALU.mult, None, [B_st], [B_c["sc"]])
    p.ts("dve", sc_sb[:, 4:8], st[:, 4:8], -4.0, None, ALU.mult, None, [B_st], [B_c["sc"]])
    p.ts("dve", hb_sb[:, 0:8], bfm_sb[:, C_BA:C_BA + 8], 0.5, None, ALU.mult, None, [B_c["bfm"]], [B_c["sc"]])
    p.act(esink_sb[:], vrow_sb[:, OFF_SINK:OFF_SINK + 8], AF.Exp, [B_c["vrow"]], [B_c["sc"]])

    CST = list(B_c.values())
    LN_ENG = "pool"

    def load_x(src_rows, nsub, nt, rope_src=None):
        for s in range(nsub):
            p.dma("sp", xtok[0:nt, s, :], src_rows(s), writes=[B_xtok[s]], chan="x%d" % s)
        if rope_src is not None:
            p.dma("sp", rope_src[0], rope_src[1], writes=[B_rope], chan="rope")

    XS = [xT, [[b_] for b_ in B_xT]]

    def transposes_x(nsub, nt, dst_xs=None, bank_fn=None, extra_w=()):
        xTv, BxTv = dst_xs if dst_xs is not None else (XS[0], XS[1])
        for s in range(nsub):
            for g in range(2):
                bank, Bb = (bank_fn or psum)()
                for j in range(4):
                    kc = g * 4 + j
                    p.tr(bank[:, j * 128:j * 128 + nt], xtok[0:nt, s, kc * 128:(kc + 1) * 128], ident_f[0:nt, 0:nt],
                         [B_xtok[s]] + CST, [Bb])
                src = bank[:].rearrange("p (a b) -> p a b", b=128)[:, :, 0:nt]
                dst = xTv[:, g * 4:(g + 1) * 4, s * nt:(s + 1) * nt]
                if g == 0:
                    p.cp("act", dst, src, [Bb], list(BxTv[s]) + list(extra_w))
                else:
                    p.cp("dve", dst, src, [Bb], list(BxTv[s]) + list(extra_w))

    def lru_chunk(c, ncol, nsub, nt, sample, with_gate, res):
        hw, tsz = (48, 16) if sample else (3, 1)
        slot = res.par
        xe, Bxe = res.slot("xe")
        gg_, Bgg = res.slot("gg")
        hh_, Bhh = res.slot("hh")
        xcb_, Bxcb = res.slot("xcb")
        xT = XS[0]
        xTr = [b_ for s_ in range(nsub) for b_ in XS[1][s_]]
        X, BX = res.tmp(0)
        Y, BY = res.tmp(1)
        p.cp("dve", xe[:, 0:hw], hist[:, c, 0:hw], [B_hist[c]], [Bxe])
        bank, Bb = res.psum()
        for k in range(8):
            p.mm(bank[:, 0:ncol], w_in_sb[:, k, c * 128:(c + 1) * 128], xT[:, k, 0:ncol], k == 0, k == 7,
                 [B_winA[k]] + xTr, [Bb])
        p.act(xe[:, hw:hw + ncol], bank[:, 0:ncol], AF.Identity, [Bb] + CST, [Bxe], bias=bfm_sb[:, C_BXR + c:C_BXR + c + 1])
        p.cp("dve", hist[:, c, 0:hw], xe[:, ncol:ncol + hw], [Bxe], [B_hist[c]])
        if with_gate:
            bank2, Bb2 = res.psum()
            for k in range(8):
                p.mm(bank2[:, 0:ncol], w_in_sb[:, k, 512 + c * 128:512 + (c + 1) * 128], xT[:, k, 0:ncol], k == 0, k == 7,
                     [B_win[k]] + xTr, [Bb2])
            gx_ = X[:, 0:ncol]
            gsl = gg_[:, 0:ncol]
            p.act(gsl, bank2[:, 0:ncol], AF.Identity, [Bb2] + CST, [Bgg],
                  bias=bfm_sb[:, C_BGATE + c:C_BGATE + c + 1])
            p.tt("dve", gx_, gsl, gsl, ALU.mult, [Bgg], [BX])
            p.ts("dve", gx_, gx_, 0.044715, 1.0, ALU.mult, ALU.add, [BX], [BX])
            p.tt("dve", gx_, gx_, gsl, ALU.mult, [BX, Bgg], [BX])
            p.act(gx_, gx_, AF.Tanh, [BX], [BX], scale=0.7978845608028654)
            p.stt(gsl, gx_, 1.0, gsl, ALU.add, ALU.mult, [BX, Bgg], [Bgg])
        xc = X[:, 512:512 + ncol]
        cw = lambda j: bfm_sb[:, C_CONVW + j * 4 + c:C_CONVW + j * 4 + c + 1]
        cacc, Bca = res.psum()
        ca = cacc[:, 0:ncol]
        p.ts("dve", ca, xe[:, 0:ncol], cw(0), bfm_sb[:, C_CONVB + c:C_CONVB + c + 1], ALU.mult, ALU.add,
             [Bxe] + CST, [Bca])
        for j in range(1, 3):
            p.stt(ca, xe[:, j * tsz:j * tsz + ncol], cw(j), ca, ALU.mult, ALU.add, [Bxe, Bca] + CST, [Bca])
        p.stt(xc, xe[:, 3 * tsz:3 * tsz + ncol], cw(3), ca, ALU.mult, ALU.add, [Bxe, Bca] + CST, [BX])
        xb = xcb_[:, 0:ncol]
        p.cp("act" if with_gate else "pool", xb, xc, [BX], [Bxcb])
        a = Y[:, 0:ncol]
        sq = X[:, 0:ncol]
        iu_ = Y[:, 512:512 + ncol]
        bR, BR = res.psum()
        p.mm(bR[:, 0:ncol], wab_sb[:, c, :], xb, True, True, [Bxcb] + CST, [BR])
        p.act(a, bR[:, 0:ncol], AF.Tanh, [BR] + CST, [BY], scale=0.5, bias=hb_sb[:, c:c + 1])
        bI, BI = res.psum()
        p.mm(bI[:, 0:ncol], wxb_sb[:, c, :], xb, True, True, [Bxcb] + CST, [BI])
        p.act(iu_, bI[:, 0:ncol], AF.Tanh, [BI] + CST, [BY], scale=0.5, bias=hb_sb[:, 4 + c:5 + c])
        p.act(sq, a, AF.Exp, [BY, BX] + CST, [BX], scale=sc_sb[:, c:c + 1], bias=sc_sb[:, c:c + 1])
        p.act(a, a, AF.Exp, [BY] + CST, [BY], scale=sc_sb[:, 4 + c:5 + c], bias=sc_sb[:, 4 + c:5 + c])
        p.act(sq, sq, AF.Relu, [BX], [BX], scale=-1.0, bias=1.0)
        p.act(sq, sq, AF.Sqrt, [BX], [BX])
        p.stt(iu_, iu_, 1.0, xc, ALU.add, ALU.mult, [BY, BX], [BY])
        p.stt(iu_, iu_, 0.5, sq, ALU.mult, ALU.mult, [BY, BX], [BY])
        hs = hh_[:, 0:ncol]
        if not sample:
            p.op("dve", lambda e: e.tensor_tensor_scan(hs, a, iu_, hcar[:, c, 0:1], ALU.mult, ALU.add),
                 [BY, B_hcar[c]], [Bhh])
            p.cp("dve", hcar[:, c, 0:1], hh_[:, ncol - 1:ncol], [Bhh], [B_hcar[c]])
        else:
            for t in range(4):
                prev = hcar[:, c, 0:16] if t == 0 else hh_[:, (t - 1) * 16:t * 16]
                cur = hh_[:, t * 16:(t + 1) * 16]
                p.tt("dve", cur, Y[:, t * 16:(t + 1) * 16], prev, ALU.mult, [BY, B_hcar[c], Bhh], [Bhh])
                p.tt("dve", cur, cur, Y[:, 512 + t * 16:512 + (t + 1) * 16], ALU.add, [BY, Bhh], [Bhh])
        return slot

    def lru_mix(c, slot, ncol, nsub, nt, ssq_bank, Bssq, res, last_chain=True):
        X, BX = res.tmp(0)
        gg_, Bgg = res.slot("gg")
        hh_, Bhh = res.slot("hh")
        yl = X[:, 512:512 + ncol]
        ysq = X[:, 0:ncol]
        p.stt(yl, gg_[:, 0:ncol], 0.5, hh_[:, 0:ncol], ALU.mult, ALU.mult, [Bgg, Bhh, BX], [BX])
        p.tt("dve", ysq, yl, yl, ALU.mult, [BX], [BX])
        p.act(mixT[:, c, 0:ncol], yl, AF.Identity, [BX] + CST, [B_mixT[c][s] for s in range(nsub)],
              scale=bfm_sb[:, C_GLRU + c:C_GLRU + c + 1])
        for s in range(nsub):
            p.mm(ssq_bank[0:nt, 2 * s:2 * s + 2], X[:, s * nt:(s + 1) * nt], ones_f[:, 0:2], False, False,
                 [BX] + CST, [Bssq])

    def qkv_sub(s, nt, vslot, kcol, res, q=True, want_k32=False, want_v32=None):
        xT = XS[0]
        xTs = list(XS[1][s])
        ta_, Bta = res.tmp(0)
        tb_, Btb = res.tmp(1)
        cosv = rope_sb[0:nt, s, 0:32]
        sinv = rope_sb[0:nt, s, 32:64]
        nsinv = rope_sb[0:nt, s, 64:96]
        RP = [B_rope]

        def rope(src, H, Bm):
            sv = src.rearrange("p (h t d) -> p h t d", t=2, d=32)
            Bv = Bm.rearrange("p (h t d) -> p h t d", t=2, d=32)
            cb = cosv.unsqueeze(1).unsqueeze(1).broadcast_to([nt, H, 2, 32])
            p.tt("dve", Bv[:, :, 0, :], sv[:, :, 1, :], nsinv.unsqueeze(1).broadcast_to([nt, H, 32]), ALU.mult,
                 [Bta] + RP, [Btb])
            p.tt("dve", Bv[:, :, 1, :], sv[:, :, 0, :], sinv.unsqueeze(1).broadcast_to([nt, H, 32]), ALU.mult,
                 [Bta] + RP, [Btb])
            p.tt("dve", sv, sv, cb, ALU.mult, [Bta] + RP, [Bta])

        if q:
            bq, Bbq = res.psum()
            for k in range(8):
                p.mm(bq[0:nt, :], xT[:, k, s * nt:(s + 1) * nt], w_in_sb[:, k, 1024:1536], k == 0, k == 7,
                     xTs + [B_win[k]], [Bbq])
            qb = ta_[0:nt, 0:512]
            p.tt("dve", qb, bq[0:nt, :], vrow_sb[0:nt, OFF_BQKV:OFF_BQKV + 512], ALU.add, [Bbq] + CST, [Bta])
        bkv, Bbkv = res.psum()
        for k in range(8):
            p.mm(bkv[0:nt, 0:256], xT[:, k, s * nt:(s + 1) * nt], w_in_sb[:, k, 1536:1792], k == 0, k == 7,
                 xTs + [B_win[k]], [Bbkv])
        kvb = ta_[0:nt, 512:768]
        p.tt("dve", kvb, bkv[0:nt, 0:256], vrow_sb[0:nt, OFF_BQKV + 512:OFF_BQKV + 768], ALU.add, [Bbkv] + CST, [Bta])
        if q:
            rope(qb, 8, tb_[0:nt, 0:512])
            p.tt("dve", qrot[0:nt, :], qb, tb_[0:nt, 0:512], ALU.add, [Bta, Btb], [B_qrot])
            pb, Bpb = res.psumb(0)
            for c in range(4):
                p.tr(pb[:, c * 128:c * 128 + nt], qrot[0:nt, c * 128:(c + 1) * 128], ident_b[0:nt, 0:nt],
                     [B_qrot] + CST, [Bpb])
            p.cp("act", qT[:, :, s * nt:(s + 1) * nt], pb.rearrange("p (a b) -> p a b", b=128)[:, :, 0:nt], [Bpb], [B_qT[s]])
        kA = ta_[0:nt, 512:640]
        kB = tb_[0:nt, 512:640]
        rope(kA, 2, kB)
        for hf_ in range(2):
            p.tt("dve", kz[0:nt, :].rearrange("p (j r) -> p j r", j=2)[:, :, hf_ * 192:hf_ * 192 + 64],
                 kA.rearrange("p (j d) -> p j d", d=64), kB.rearrange("p (j d) -> p j d", d=64),
                 ALU.add, [Bta, Btb], [B_kdup])
        if want_k32:
            p.tt("dve", k32[0:nt, :], kA, kB, ALU.add, [Bta, Btb], [B_k32])
        p.cp("act", vaug[0:nt, vslot, :, 0:64], ta_[0:nt, 640:768].rearrange("p (j d) -> p j d", d=64), [Bta], [B_vaug[vslot]])
        if want_v32 is not None:
            want_v32(ta_[0:nt, 640:768], Bta)
        pb, Bpb = res.psumb(1)
        for j in range(4):
            p.tr(pb[:, j * 128:j * 128 + nt], kz[0:nt, j * 128:(j + 1) * 128], ident_b[0:nt, 0:nt], [B_kdup] + CST, [Bpb])
        p.cp("dve", kT[:, :, kcol:kcol + nt], pb[:, 0:512].rearrange("p (a b) -> p a b", b=128)[:, :, 0:nt],
             [Bpb], [B_kT[s + 1]])

    def attn_tail(s, nt, ob, res):
        par = res.par
        so = res.sto
        Bst = B_st2[res.idx]
        yat_, Byat = res.tmp(0)
        yat = yat_[0:nt, 0:512]
        for j in range(2):
            ov = ob[j][0][0:nt, 0:260].rearrange("p (g d) -> p g d", d=65)
            p.tt("dve", st[0:nt, so + 8 + 4 * j:so + 12 + 4 * j], ov[:, :, 64], esink_sb[0:nt, 4 * j:4 * j + 4], ALU.add,
                 [ob[j][1]] + CST, [Bst])
        p.op("dve", lambda e: e.reciprocal(st[0:nt, so + 16:so + 24], st[0:nt, so + 8:so + 16]), [Bst], [Bst])
        for j in range(2):
            ov = ob[j][0][0:nt, 0:260].rearrange("p (g d) -> p g d", d=65)
            p.tt("dve", yat[:, 256 * j:256 * (j + 1)].rearrange("p (g d) -> p g d", d=64), ov[:, :, 0:64],
                 st[0:nt, so + 16 + 4 * j:so + 20 + 4 * j].unsqueeze(2).broadcast_to([nt, 4, 64]), ALU.mult,
                 [ob[j][1], Bst], [Byat])
        p.stt(yat_[0:nt, 512:1024], yat, 1.0, yat, ALU.mult, ALU.mult, [Byat], [Byat, Bst], accum_out=st[0:nt, so + 24:so + 25])
        p.ts("dve", st[0:nt, so + 25:so + 26], st[0:nt, so + 24:so + 25], 1.0 / 512, EPS_RMS, ALU.mult, ALU.add, [Bst], [Bst])
        p.act(st[0:nt, so + 26:so + 27], st[0:nt, so + 25:so + 26], AF.Sqrt, [Bst], [Bst])
        p.op("dve", lambda e: e.reciprocal(rstd_sb[0:nt, 4 + s:5 + s], st[0:nt, so + 26:so + 27]), [Bst], [B_rstda[s]])
        ynp = yn[0:nt, par, :]
        p.tt("dve", ynp, yat, vrow_sb[0:nt, OFF_GATT:OFF_GATT + 512], ALU.mult, [Byat] + CST, [B_yn[par]])
        pb, Bpb = res.psumb(0)
        for c in range(4):
            p.tr(pb[:, c * 128:c * 128 + nt], yn[0:nt, par, c * 128:(c + 1) * 128], ident_b[0:nt, 0:nt], [B_yn[par]] + CST, [Bpb])
        p.cp("act", mixT[:, 4:8, s * nt:(s + 1) * nt], pb.rearrange("p (a b) -> p a b", b=128)[:, :, 0:nt],
             [Bpb], [B_mixT[4 + c][s] for c in range(4)])

    def attention_j(s, keyblocks, res):
        nt = 128
        par = res.par
        so = res.sto
        Bst = B_st2[res.idx]
        stb = (0, 1) if par == 0 else (2, 3)
        obi = 4 + par
        yat_, Byat = res.tmp(0)
        yat = yat_[0:nt, 0:512]
        nkb = len(keyblocks)
        it = 0
        for j in range(2):
            ob, Bob = psf[obi], B_psf[obi]
            p.mm(ob[0:nt, 0:260], zero_b[:, 0:nt], zero_b[:, 0:260], True, False, CST, [Bob])
            for ib, (nk, kTfn, vfn, mask, rds) in enumerate(keyblocks):
                ring = 2 * par + (it % 2)
                bank, Bb = psf[stb[it % 2]], B_psf[stb[it % 2]]
                it += 1
                pT = pT_all[:, 2 * ring, :].rearrange("p (h q) -> p h q", q=128)
                BpT = B_pT[ring]
                for g in range(4):
                    h = 4 * j + g
                    c, hf = h // 2, h % 2
                    p.mm(bank[0:nk, g * nt:(g + 1) * nt], kTfn(j, hf), qT[:, c, s * nt:(s + 1) * nt],
                         True, True, rds + [B_qT[s]], [Bb])
                pv = pT[0:nk, :, :]
                p.act(pv, bank[0:nk, 0:4 * nt].rearrange("p (g q) -> p g q", q=nt), AF.Exp, [Bb], [BpT], scale=SCALE)
                p.tt("dve", pv, pv, mask.unsqueeze(1).broadcast_to([nk, 4, nt]), ALU.mult, [BpT] + CST, [BpT])
                for g in range(4):
                    p.mm(ob[0:nt, g * 65:(g + 1) * 65], pT[0:nk, g, :], vfn(j), False, ib == nkb - 1 and g == 3,
                         rds + [BpT], [Bob])
            ov = ob[0:nt, 0:260].rearrange("p (g d) -> p g d", d=65)
            p.tt("dve", st[0:nt, so + 8:so + 12], ov[:, :, 64], esink_sb[0:nt, 4 * j:4 * j + 4], ALU.add, [Bob] + CST, [Bst])
            p.op("dve", lambda e: e.reciprocal(st[0:nt, so + 16:so + 20], st[0:nt, so + 8:so + 12]), [Bst], [Bst])
            p.tt("dve", yat[:, 256 * j:256 * (j + 1)].rearrange("p (g d) -> p g d", d=64), ov[:, :, 0:64],
                 st[0:nt, so + 16:so + 20].unsqueeze(2).broadcast_to([nt, 4, 64]), ALU.mult, [Bob, Bst], [Byat])
        p.stt(yat_[0:nt, 512:1024], yat, 1.0, yat, ALU.mult, ALU.mult, [Byat], [Byat, Bst], accum_out=st[0:nt, so + 24:so + 25])
        p.ts("dve", st[0:nt, so + 25:so + 26], st[0:nt, so + 24:so + 25], 1.0 / 512, EPS_RMS, ALU.mult, ALU.add, [Bst], [Bst])
        p.act(st[0:nt, so + 26:so + 27], st[0:nt, so + 25:so + 26], AF.Sqrt, [Bst], [Bst])
        p.op("dve", lambda e: e.reciprocal(rstd_sb[0:nt, 4 + s:5 + s], st[0:nt, so + 26:so + 27]), [Bst], [B_rstda[s]])
        ynp = yn[0:nt, par, :]
        p.tt("dve", ynp, yat, vrow_sb[0:nt, OFF_GATT:OFF_GATT + 512], ALU.mult, [Byat] + CST, [B_yn[par]])
        pb, Bpb = res.psumb(0)
        for c in range(4):
            p.tr(pb[:, c * 128:c * 128 + nt], yn[0:nt, par, c * 128:(c + 1) * 128], ident_b[0:nt, 0:nt], [B_yn[par]] + CST, [Bpb])
        p.cp("act", mixT[:, 4:8, s * nt:(s + 1) * nt], pb.rearrange("p (a b) -> p a b", b=128)[:, :, 0:nt],
             [Bpb], [B_mixT[4 + c][s] for c in range(4)])

    def layernorm(buf_ap, Bbuf, nt, og, ob_, res, bias_eng=None):
        par = res.par
        so = res.sto
        Bst = B_st2[res.idx]
        for hlf in range(2):
            p.op("dve", lambda e, hlf=hlf: e.bn_stats(bnst[0:nt, res.idx, hlf, :], buf_ap[0:nt, hlf * 512:(hlf + 1) * 512]),
                 [Bbuf], [B_bnst[res.idx]])
        p.op("dve", lambda e: e.bn_aggr(st[0:nt, so + 28:so + 30], bnst[0:nt, res.idx, :, :].rearrange("p a b -> p (a b)")),
             [B_bnst[res.idx]], [Bst])
        p.ts("dve", st[0:nt, so + 30:so + 31], st[0:nt, so + 29:so + 30], EPS_LN, None, ALU.add, None, [Bst], [Bst])
        p.act(st[0:nt, so + 31:so + 32], st[0:nt, so + 30:so + 31], AF.Sqrt, [Bst], [Bst])
        p.op("dve", lambda e: e.reciprocal(st[0:nt, so + 32:so + 33], st[0:nt, so + 31:so + 32]), [Bst], [Bst])
        p.stt(st[0:nt, so + 33:so + 34], st[0:nt, so + 28:so + 29], -1.0, st[0:nt, so + 32:so + 33], ALU.mult, ALU.mult, [Bst], [Bst])
        p.act(buf_ap[0:nt, :], buf_ap[0:nt, :], AF.Identity, [Bbuf, Bst], [Bbuf], scale=st[0:nt, so + 32:so + 33],
              bias=st[0:nt, so + 33:so + 34])
        p.tt(LN_ENG, buf_ap[0:nt, :], buf_ap[0:nt, :], vrow_sb[0:nt, og:og + 1024], ALU.mult, [Bbuf] + CST, [Bbuf])
        p.tt(bias_eng or LN_ENG, buf_ap[0:nt, :], buf_ap[0:nt, :], vrow_sb[0:nt, ob_:ob_ + 1024], ALU.add, [Bbuf] + CST, [Bbuf])

    def outproj_ln1(s, nt, res):
        for dh in range(2):
            hv = h1[0:nt, s, dh * 512:(dh + 1) * 512]
            bA, BA = res.psum()
            for c in range(4):
                p.mm(bA[0:nt, :], mixT[:, c, s * nt:(s + 1) * nt], w_out_sb[:, c, dh * 512:(dh + 1) * 512], c == 0, c == 3,
                     [B_mixT[c][s], B_wout[0]], [BA])
            p.stt(hv, bA[0:nt, :], rstd_sb[0:nt, s:s + 1], vrow_sb[0:nt, OFF_BOUT + dh * 512:OFF_BOUT + (dh + 1) * 512],
                  ALU.mult, ALU.add, [BA, B_rstd] + CST, [B_h1[s]])
            bB, BB = res.psum()
            for c in range(4, 8):
                p.mm(bB[0:nt, :], mixT[:, c, s * nt:(s + 1) * nt], w_out_sb[:, c, dh * 512:(dh + 1) * 512], c == 4, c == 7,
                     [B_mixT[c][s], B_wout[1]], [BB])
            p.stt(hv, bB[0:nt, :], rstd_sb[0:nt, 4 + s:5 + s], hv, ALU.mult, ALU.add, [BB, B_rstda[s], B_h1[s]], [B_h1[s]])
            p.stt(hv, xtok[0:nt, s, dh * 512:(dh + 1) * 512], ALPHA, hv, ALU.mult, ALU.add, [B_xtok[s], B_h1[s]], [B_h1[s]])
        layernorm(h1[:, s, :], B_h1[s], nt, OFF_L1G, OFF_L1B, res)
        for g in range(2):
            bank, Bb = res.psum()
            for j in range(4):
                kc = g * 4 + j
                p.tr(bank[:, j * 128:j * 128 + nt], h1[0:nt, s, kc * 128:(kc + 1) * 128], ident_f[0:nt, 0:nt],
                     [B_h1[s]] + CST, [Bb])
            src = bank[:].rearrange("p (a b) -> p a b", b=128)[:, :, 0:nt]
            p.cp("act", h1T[:, g * 4:(g + 1) * 4, s * nt:(s + 1) * nt], src, [Bb], [B_h1T[s]])

    wgu_n = [0]
    wdn_n = [0]

    def gu_load(f):
        sl = wgu_n[0] % 3
        wgu_n[0] += 1
        p.dma("pool", wgu[:, sl, 0, :, :], wg_b[:, f * 128:(f + 1) * 128].rearrange("(k p) n -> p k n", p=128),
              reads=[B_wgb], writes=[B_wg[sl]], chan="wg%d" % sl)
        p.dma("pool", wgu[:, sl, 1, :, :], wu_b[:, f * 128:(f + 1) * 128].rearrange("(k p) n -> p k n", p=128),
              reads=[B_wub], writes=[B_wu[sl]], chan="wu%d" % sl)

    def dn_load(i):
        dh, fp = i // (NF // 2), i % (NF // 2)
        sl = wdn_n[0] % 3
        wdn_n[0] += 1
        p.dma("pool", wdn[:, sl, :, :],
              wd_b[fp * 256:(fp + 1) * 256, dh * 512:(dh + 1) * 512].rearrange("(f p) n -> p f n", p=128),
              reads=[B_wdb], writes=[B_wdn[sl]], chan="wd%d" % sl)

    def ffn_weight_loads():
        base = (wgu_n[0], wdn_n[0])
        for f in range(3):
            gu_load(f)
        for i in range(3):
            dn_load(i)
        return base

    def ffn(ncol, nsub, nt, base, store, prefetch=None, head=None, tail=False):
        gbase, dbase = base
        h1Tr = B_h1T[0:nsub]
        for f in range(NF):
            sl = (gbase + f) % 3
            bG, BG = psum6()
            for k in range(8):
                p.mm(bG[:, 0:ncol], wgu[:, sl, 0, k, :], h1T[:, k, 0:ncol], k == 0, k == 7, [B_wg[sl]] + h1Tr, [BG])
            bU, BU = psum6()
            for k in range(8):
                p.mm(bU[:, 0:ncol], wgu[:, sl, 1, k, :], h1T[:, k, 0:ncol], k == 0, k == 7, [B_wu[sl]] + h1Tr, [BU])
            if f + 3 < NF:
                gu_load(f + 3)
            sg_, Bsg = gettmp()
            p.act(sg_[:, 0:ncol], bG[:, 0:ncol], AF.Tanh, [BG], [Bsg], scale=0.5)
            p.stt(sg_[:, 0:ncol], sg_[:, 0:ncol], 1.0, bG[:, 0:ncol], ALU.add, ALU.mult, [Bsg, BG], [Bsg])
            p.stt(hidT[:, f, 0:ncol], sg_[:, 0:ncol], 0.5, bU[:, 0:ncol], ALU.mult, ALU.mult, [Bsg, BU], [B_hid[f]])
        if prefetch is not None:
            prefetch()
        i = 0
        for dh in range(2):
            accs = [psum() for _ in range(nsub)]
            for fp in range(NF // 2):
                sl = (dbase + i) % 3
                for f2 in range(2):
                    f = 2 * fp + f2
                    for s in range(nsub):
                        p.mm(accs[s][0][0:nt, :], hidT[:, f, s * nt:(s + 1) * nt], wdn[:, sl, f2, :], f == 0, f == NF - 1,
                             [B_hid[f], B_wdn[sl]], [accs[s][1]])
                if i + 3 < NF:
                    dn_load(i + 3)
                i += 1
            for s in range(nsub):
                hv = h1[0:nt, s, dh * 512:(dh + 1) * 512]
                p.stt(hv, hv, ALPHA, accs[s][0][0:nt, :], ALU.mult, ALU.add, [B_h1[s], accs[s][1]], [B_h1[s]])
        def ln2_chain(s):
            layernorm(h1[:, s, :], B_h1[s], nt, OFF_L2G, OFF_L2B, CH4[s], bias_eng="dve" if tail else None)
            store(s)

        p.merge([p.record(lambda s=s: ln2_chain(s)) for s in range(nsub)] + ([p.record(head)] if head is not None else []))

    def fm_to_tm(src_fn, ncols, dst_dram, chan):
        bank, Bb = psum()
        for c in range(4):
            ap, rds = src_fn(c)
            p.tr(bank[0:ncols, c * 128:(c + 1) * 128], ap, ident_f[:, :], rds + CST, [Bb])
        o_, Bo = gettmp()
        p.cp("dve", o_[0:ncols, 0:512], bank[0:ncols, :], [Bb], [Bo])
        p.dma("sp", dst_dram, o_[0:ncols, 0:512], reads=[Bo], chan=chan)

    def carry_kv():
        p.cp("act", kT[:, :, 0:128], kT[:, :, 512:640], [B_kT[4]], [B_kT[0]])
        if stage >= 13:
            p.cp("act", vaug[:, 0, :, :], vaug[:, 4, :, :], [B_vaug[4]], [B_vaug[0]])

    xT_alt = h1[:, 0:2, :].rearrange("p a b -> p (a b)").bitcast(BF16).rearrange("p (k n) -> p k n", k=8)
    XS_ALT = (xT_alt, [[B_h1[0], B_h1[1]]] * 4)
    XS_MAIN = (xT, [[b_] for b_ in B_xT])
    pre_bank = [0]

    def pre_psum():
        i = 4 + pre_bank[0] % 2
        pre_bank[0] += 1
        return psf[i], B_psf[i]

    def prefix_pass(n_pre):
        xs = [XS_MAIN, XS_ALT]
        t0 = 8 - n_pre

        def ldx(t, last):
            load_x(lambda s: xpre[t * 512 + s * 128:t * 512 + (s + 1) * 128, :], 4, 128,
                   (rope_sb[:, 3, :], ropex_d) if last else None)

        ldx(t0, n_pre == 1)
        transposes_x(4, 128, dst_xs=xs[0])
        for i in range(n_pre):
            last = i == n_pre - 1
            XS[0], XS[1] = xs[i % 2]
            chains = [p.record(lambda c=c: lru_chunk(c, 512, 4, 128, False, False, CHP[c])) for c in range(4)]
            if not last:
                ldx(t0 + i + 1, i + 1 == n_pre - 1)
                chains.append(p.record(lambda: transposes_x(4, 128, dst_xs=xs[(i + 1) % 2], bank_fn=pre_psum)))
            p.merge(chains)
            if last:
                qkv_sub(3, 128, 4, 512, CH_S, q=False)
                carry_kv()
        XS[0], XS[1] = XS_MAIN

    def rstd_lru(ssq_bank, Bssq, nsub, nt):
        p.mm(ssq_bank[0:nt, 0:8], zero_f[:, 0:nt], ones_f[:, 0:8], False, True, CST, [Bssq])
        v = ssq_bank[0:nt, 0:2 * nsub].rearrange("p (s two) -> p s two", two=2)[:, :, 0]
        p.ts("dve", st[0:nt, 160:160 + nsub], v, 1.0 / 512, EPS_RMS, ALU.mult, ALU.add, [Bssq], [B_st])
        p.act(st[0:nt, 164:164 + nsub], st[0:nt, 160:160 + nsub], AF.Sqrt, [B_st], [B_st])
        p.op("dve", lambda e: e.reciprocal(rstd_sb[0:nt, 0:nsub], st[0:nt, 164:164 + nsub]), [B_st], [B_rstd])

    def load_main_x(t):
        load_x(lambda s: xmain[t * 512 + s * 128:t * 512 + (s + 1) * 128, :], 4, 128, (rope_sb[:], ropem_d[t]))

    def main_tile(t, last, preloaded, head_done=False):
        p.fence(ARENA_B)
        base = ffn_weight_loads()
        if not preloaded:
            load_main_x(t)
        if not head_done:
            transposes_x(4, 128)
        ssq_bank, Bssq = psf[5], B_psf[5]
        p.mm(ssq_bank[0:128, 0:8], zero_f[:, 0:128], ones_f[:, 0:8], True, False, CST, [Bssq])
        ckp("m:ssqzero")
        def lru_super(cs, res):
            for c in cs:
                slot = lru_chunk(c, 512, 4, 128, False, True, res)
                lru_mix(c, slot, 512, 4, 128, ssq_bank, Bssq, res)

        def qkv_super():
            for s in range(4):
                lastsub = last and s == 3

                def v32out(ap, Bt):
                    p.dma("sp", vp, ap, reads=[Bt], chan="vp")

                qkv_sub(s, 128, s + 1, 128 + s * 128, CH_Q, q=True, want_k32=lastsub, want_v32=v32out if lastsub else None)
                if lastsub:
                    p.dma("sp", kp, k32[:, :], reads=[B_k32], chan="kp")

        p.merge([p.record(lambda: lru_super((0, 2), CH_LA)), p.record(lambda: lru_super((1, 3), CH_LB)),
                 p.record(qkv_super)])
        rstd_lru(ssq_bank, Bssq, 4, 128)
        p.fence(B_tq)
        ckp("m:rstd_lru")
        def attn_chain(s, i):
            kbs = [
                (128, (lambda j, hf: kT[:, 2 * j + hf, s * 128:s * 128 + 128]),
                 (lambda j: vaug[:, s, j, :]), maskp_sb[:, 0, :], [B_kT[s], B_vaug[s]]),
                (128, (lambda j, hf: kT[:, 2 * j + hf, 128 + s * 128:256 + s * 128]),
                 (lambda j: vaug[:, s + 1, j, :]), maskp_sb[:, 1, :], [B_kT[s + 1], B_vaug[s + 1]]),
            ]
            attention_j(s, kbs, CH[i])

        for s0 in (0, 2):
            p.merge([p.record(lambda s=s0 + i, i=i: attn_chain(s, i)) for i in range(2)])
        carry_kv()
        p.merge([p.record(lambda s=s: outproj_ln1(s, 128, CH4[s])) for s in range(4)], skew=True)
        if last:
            fm_to_tm(lambda c: (hist[:, c, 0:3], [B_hist[c]]), 3, convp, "convp")
            fm_to_tm(lambda c: (hcar[:, c, 0:1], [B_hcar[c]]), 1, lrup, "lrup")
        p.fence(ARENA_A)

        def store(s):
            p.dma("sp", yp[t * 512 + s * 128:t * 512 + (s + 1) * 128, :], h1[:, s, :], reads=[B_h1[s]], chan="y%d" % s)

        ffn(512, 4, 128, base, store, prefetch=None if last else (lambda: load_main_x(t + 1)),
            head=None if last else (lambda: transposes_x(4, 128, extra_w=B_hid[0:8])), tail=last)
        ckp("m:ffn")

    def apply_flag():
        for c in range(4):
            p.ts("dve", hcar[:, c, 0:1], hcar[:, c, 0:1], flag_sb[:, 0:1], None, ALU.mult, None, [B_hcar[c]] + CST, [B_hcar[c]])
            p.ts("dve", hist[:, c, 0:3], hist[:, c, 0:3], flag_sb[:, 0:1], None, ALU.mult, None, [B_hist[c]] + CST, [B_hist[c]])
        if stage >= 14:
            v0 = vaug[:, 0, :, :].rearrange("p j d -> p (j d)")
            p.tt("dve", v0, v0, flag_sb[:, 0:1].to_broadcast([128, 130]), ALU.mult, [B_vaug[0]] + CST, [B_vaug[0]])

    def sample_tile(prefetch, head=None):
        nt = NS
        p.fence(ARENA_B)
        base = ffn_weight_loads()
        load_x(lambda s: xs, 1, nt, (rope_sb[0:nt, 0, :], ropes_d))
        sc_t, Bsc = gettmp()
        p.dma("sp", sc_t[0:48, 0:512], sconv, writes=[Bsc], chan="sconv")
        p.dma("sp", sc_t[0:16, 512:1024], slru, writes=[Bsc], chan="slru")
        ckst = h1[:, 0:2, :].rearrange("p a b -> p (a b)").rearrange("p (b c) -> p b c", c=128)
        cvst = h1[:, 2:4, :].rearrange("p a b -> p (a b)").rearrange("p (b c) -> p b c", c=128)
        p.dma("sp", ckst, ck.rearrange("b w c -> w b c"), writes=[B_h1[0], B_h1[1]], chan="ck")
        p.dma("sp", cvst, cv.rearrange("b w c -> w b c"), writes=[B_h1[2], B_h1[3]], chan="cv")
        p.dma("sp", ks[:, 0:124, :], ck[:, 4:128, :], chan="ksc")
        p.dma("sp", vs[:, 0:124, :], cv[:, 4:128, :], chan="vsc")
        transposes_x(1, nt)
        for c in range(4):
            bank, Bb = psum()
            p.tr(bank[:, 0:48], sc_t[0:48, c * 128:(c + 1) * 128], ident_f[0:48, 0:48], [Bsc] + CST, [Bb])
            p.tr(bank[:, 64:80], sc_t[0:16, 512 + c * 128:512 + (c + 1) * 128], ident_f[0:16, 0:16], [Bsc] + CST, [Bb])
            p.cp("dve", hist[:, c, 0:48], bank[:, 0:48], [Bb], [B_hist[c]])
            p.cp("dve", hcar[:, c, 0:16], bank[:, 64:80], [Bb], [B_hcar[c]])
        kcz = h1T[:, 0:2, :].rearrange("p a b -> p (a b)").rearrange("p (r x) -> p r x", r=2)
        kTc = hidT[:, 20:22, :].rearrange("p a b -> p (a b)").rearrange("p (r j w) -> p r j w", r=2, w=128)
        p.memset("dve", kcz, 0.0, [B_kcd])
        vaugc = xtok[:, 1:3, :].rearrange("p a b -> p (a b)").bitcast(BF16)[:, 0:NB * 2 * 65].rearrange(
            "p (b j d) -> p b j d", j=2, d=65)
        B_vc = [B_xtok[1], B_xtok[2]]
        p.cp("act", vaugc[:, :, :, 0:64], cvst.rearrange("p b (j d) -> p b j d", d=64), [B_h1[2], B_h1[3]], B_vc)
        p.memset("dve", vaugc[:, :, :, 64:65], 1.0, B_vc)
        ssq_bank, Bssq = psf[5], B_psf[5]
        p.mm(ssq_bank[0:nt, 0:8], zero_f[:, 0:nt], ones_f[:, 0:8], True, False, CST, [Bssq])
        def s_chain(c):
            slot = lru_chunk(c, nt, 1, nt, True, True, CH[c % 2])
            p.cp("dve", hcar[:, c, 0:16], hh[:, slot, 48:64], [B_hh[slot]], [B_hcar[c]])
            lru_mix(c, slot, nt, 1, nt, ssq_bank, Bssq, CH[c % 2], last_chain=(c == 3))

        for c0 in (0, 2):
            p.merge([p.record(lambda c=c0 + i: s_chain(c)) for i in range(2)])
        rstd_lru(ssq_bank, Bssq, 1, nt)
        fm_to_tm(lambda c: (hist[:, c, 0:48], [B_hist[c]]), 48, convs, "convs")
        fm_to_tm(lambda c: (hcar[:, c, 0:16], [B_hcar[c]]), 16, lrus, "lrus")

        def v32out(ap, Bt):
            for t in range(4):
                p.dma("sp", vs[:, 124 + t, :], ap[t * 16:(t + 1) * 16, :], reads=[Bt], chan="vs%d" % t)

        qkv_sub(0, nt, 1, 128, CH_S, q=True, want_k32=True, want_v32=v32out)
        for t in range(4):
            p.dma("sp", ks[:, 124 + t, :], k32[t * 16:(t + 1) * 16, :], reads=[B_k32], chan="ks%d" % t)
        B_kcd2 = [B_kcd, p.buf("kcd1")]
        ob = [(psf[4], B_psf[4]), (psf[5], B_psf[5])]
        for j in range(2):
            p.mm(ob[j][0][0:nt, 0:260], zero_b[:, 0:nt], zero_b[:, 0:260], True, False, CST, [ob[j][1]])

        def block(ch, nk, kTfn, vfn, mask, rds):
            bank, Bb = psf[ch], B_psf[ch]
            pT = pT_all[:, 2 * ch, :].rearrange("p (h q) -> p h q", q=nt)
            BpT = B_pT[ch]
            for h in range(8):
                p.mm(bank[0:nk, h * nt:(h + 1) * nt], kTfn(h // 4, h % 2), qT[:, h // 2, 0:nt], True, True, rds + [B_qT[0]], [Bb])
            pv = pT[0:nk, :, :]
            p.act(pv, bank[0:nk, 0:8 * nt].rearrange("p (g q) -> p g q", q=nt), AF.Exp, [Bb], [BpT], scale=SCALE)
            p.tt("dve", pv, pv, mask.unsqueeze(1).broadcast_to([nk, 8, nt]), ALU.mult, [BpT] + CST, [BpT])
            for h in range(8):
                j = h // 4
                p.mm(ob[j][0][0:nt, (h % 4) * 65:(h % 4 + 1) * 65], pT[0:nk, h, :], vfn(j), False, False,
                     rds + [BpT], [ob[j][1]])

        def chain(ch):
            for b in range(ch, NB, 2):
                for hf_ in range(2):
                    p.cp("act", kcz[:, ch, :].rearrange("p (j x) -> p j x", j=2)[:, :, hf_ * 192:hf_ * 192 + 64],
                         ckst[:, b, :].rearrange("p (j d) -> p j d", d=64), [B_h1[0], B_h1[1]], [B_kcd2[ch]])
                pb, Bpb = psbs[ch][:, 0:512], B_psb[ch]
                for j in range(4):
                    p.tr(pb[:, j * 128:(j + 1) * 128], kcz[:, ch, j * 128:(j + 1) * 128], ident_b[:, :], [B_kcd2[ch]] + CST, [Bpb])
                p.cp("dve", kTc[:, ch, :, :], pb[:, 0:512].rearrange("p (j w) -> p j w", w=128), [Bpb], [B_kTc[ch]])
                block(ch, 128, (lambda j, hf: kTc[:, ch, 2 * j + hf, :]), (lambda j, b=b: vaugc[:, b, j, :]),
                      masks_sb[:, b, :], [B_kTc[ch]] + B_vc)
            if ch == 1:
                block(ch, nt, (lambda j, hf: kT[:, 2 * j + hf, 128:128 + nt]), (lambda j: vaug[0:nt, 1, j, :]),
                      masks_sb[0:nt, 16, :], [B_kT[1], B_vaug[1]])

        p.merge([p.record(lambda: chain(0)), p.record(lambda: chain(1))])
        for j in range(2):
            p.mm(ob[j][0][0:nt, 0:260], zero_b[:, 0:nt], zero_b[:, 0:260], False, True, CST, [ob[j][1]])
        attn_tail(0, nt, ob, CH[0])
        p.fence(B_kcd2)
        outproj_ln1(0, nt, CH[0])
        p.fence(ARENA_A + B_vc)

        def store(s):
            p.dma("sp", ys, h1[0:nt, 0, :], reads=[B_h1[0]], chan="y0")

        ffn(nt, 1, nt, base, store, prefetch=prefetch, head=head)

    def program():
      if n_pre > 0:
          prefix_pass(n_pre)
      if n_pre > 0:
          p.fence(PREFIX_ALIAS + B_gg)
      preloaded = False
      if do_sample:
          keep_k = mixT[:, 0:4, 128:256]
          keep_v = mixT[:, 4:6, 128:193]
          keep_s = st[:, 168:192]
          Bkk = p.buf("keepk")
          p.cp("act", keep_k, kT[:, :, 0:128], [B_kT[0]], [Bkk])
          p.cp("act", keep_v, vaug[:, 0, :, :], [B_vaug[0]], [Bkk])
          p.cp("dve", keep_s[:, 0:12].rearrange("p (c j) -> p c j", j=3), hist[:, :, 0:3], B_hist, [Bkk])
          p.cp("dve", keep_s[:, 16:20], hcar[:, :, 0], B_hcar, [Bkk])
          sample_tile((lambda: load_main_x(0)) if n_main > 0 else None,
                      head=(lambda: transposes_x(4, 128, extra_w=B_hid[0:8])) if n_main > 0 else None)
          preloaded = n_main > 0
          p.cp("dve", hist[:, :, 0:3], keep_s[:, 0:12].rearrange("p (c j) -> p c j", j=3), [Bkk], B_hist)
          p.cp("dve", hcar[:, :, 0], keep_s[:, 16:20], [Bkk], B_hcar)
          p.cp("act", kT[:, :, 0:128], keep_k, [Bkk], [B_kT[0]])
          p.cp("act", vaug[:, 0, :, :], keep_v, [Bkk], [B_vaug[0]])
      apply_flag()
      for t in range(n_main):
          main_tile(t, t == n_main - 1, preloaded, head_done=(t > 0 or (do_sample and n_main > 0)))
          preloaded = t < n_main - 1


    try:
        program()
    except Cut:
        pass
    if dbg:
        print("checkpoints:", len(cks))
    p.emit()
    es.close()
    return nc


_CACHE = {}


def _consts():
    if "c" in _CACHE:
        return _CACHE["c"]
    ident = np.eye(128, dtype=np.float32)
    sp = np.arange(128)[:, None]
    qp = np.arange(128)[None, :]
    maskp = np.stack([(sp >= qp), (sp <= qp)], axis=1).astype(np.float32)
    masks = np.zeros((128, 17, 64), np.float32)
    tq = np.arange(64) // 16
    bq = np.arange(64) % 16
    w = np.arange(128)
    for b in range(16):
        masks[:, b, :] = ((bq[None, :] == b) & (w[:, None] >= tq[None, :])).astype(np.float32)
    tk = np.arange(64) // 16
    bk = np.arange(64) % 16
    masks[0:64, 16, :] = ((bk[:, None] == bq[None, :]) & (tk[:, None] <= tq[None, :])).astype(np.float32)
    inv = 10000.0 ** (-np.arange(32, dtype=np.float64) / 32.0)

    def table(pos):
        ang = pos.astype(np.float64)[:, None] * inv[None, :]
        c = np.cos(ang).astype(np.float32)
        s = np.sin(ang).astype(np.float32)
        return np.concatenate([c, s, -s], axis=1).astype(np.float32)

    _CACHE["c"] = (ident, maskp, masks, table)
    return _CACHE["c"]


def _get_nc():
    if "nc" not in _CACHE:
        _CACHE["nc"] = build_program()
    return _CACHE["nc"]


def _prep_inputs(inp):
    ident, maskp, masks, table = _consts()
    f = lambda a: np.ascontiguousarray(np.asarray(a, dtype=np.float32))
    x_prompt = f(inp["x_prompt"])
    x_sample = f(inp["x_sample"])
    ck = f(inp["cache_k_win"])[0].reshape(128, 128, 128)
    cv = f(inp["cache_v_win"])[0].reshape(128, 128, 128)
    sconv = f(inp["state_conv"])[0]
    slru = f(inp["state_lru"])[0]
    w_in = f(inp["w_in"])[0]
    b_in = f(inp["b_in"])[0]
    conv_w = f(inp["conv_w"])[0]
    conv_b = f(inp["conv_b"])[0]
    w_a = f(inp["w_a"])[0]
    w_x = f(inp["w_x"])[0]
    b_a = f(inp["b_a"])[0].reshape(512)
    b_x = f(inp["b_x"])[0].reshape(512)
    lam = f(inp["lru_lambda"])[0]
    g_lru = f(inp["g_lru"])[0]

    def blockdiag(wm):
        o = np.zeros((128, 4, 128), np.float32)
        for c in range(4):
            for u in range(2):
                o[u * 64:(u + 1) * 64, c, u * 64:(u + 1) * 64] = wm[2 * c + u]
        return o

    fm = lambda v: np.ascontiguousarray(v.reshape(4, 128).T)
    bfm = np.zeros((128, NBF), np.float32)
    bfm[:, C_BXR:C_BXR + 4] = fm(b_in[0:512])
    bfm[:, C_BGATE:C_BGATE + 4] = fm(b_in[512:1024])
    for j in range(4):
        bfm[:, C_CONVW + 4 * j:C_CONVW + 4 * j + 4] = fm(conv_w[j])
    bfm[:, C_CONVB:C_CONVB + 4] = fm(conv_b)
    bfm[:, C_BA:C_BA + 4] = fm(b_a)
    bfm[:, C_BX:C_BX + 4] = fm(b_x)
    bfm[:, C_LAM:C_LAM + 4] = fm(lam)
    bfm[:, C_GLRU:C_GLRU + 4] = fm(g_lru)
    vrow = np.concatenate([b_in[1024:1792], f(inp["g_attn"])[0], f(inp["b_out"])[0], f(inp["ln1_g"])[0],
                           f(inp["ln1_b"])[0], f(inp["ln2_g"])[0], f(inp["ln2_b"])[0], f(inp["sinks"])[0]])[None, :]
    vrow = np.ascontiguousarray(vrow.astype(np.float32))
    shared = dict(w_in=w_in, w_out=f(inp["w_out"])[0], w_gate=f(inp["w_gate"])[0], w_up=f(inp["w_up"])[0],
                  w_down=f(inp["w_down"])[0], wab=blockdiag(w_a), wxb=blockdiag(w_x), bfm=bfm, vrow=vrow,
                  maskp=maskp, masks=masks, ident=ident)
    ropes = table(PAST + np.arange(64) // 16)
    in_maps = []
    for c in range(NCORES):
        b, hf = c // 2, c % 2
        xm = x_prompt[b, hf * HALF:(hf + 1) * HALF]
        xp = x_prompt[b, 0:HALF]
        pos0 = hf * HALF
        ropem = table(pos0 + np.arange(HALF)).reshape(8, 4, 128, 96).transpose(0, 2, 1, 3)
        ropex = table(np.maximum(pos0 - 128 + np.arange(128), 0))
        bs = slice(c * NB, (c + 1) * NB)
        m = dict(shared)
        m.update(
            xpre=np.ascontiguousarray(xp), xmain=np.ascontiguousarray(xm),
            xs=np.ascontiguousarray(x_sample[bs].transpose(1, 0, 2).reshape(NS, D)),
            ck=np.ascontiguousarray(ck[bs]), cv=np.ascontiguousarray(cv[bs]),
            sconv=np.ascontiguousarray(sconv[bs].transpose(1, 0, 2).reshape(48, LRUW)),
            slru=np.ascontiguousarray(slru[bs]),
            ropem=np.ascontiguousarray(ropem), ropex=np.ascontiguousarray(ropex), ropes=ropes,
            flag=np.full((128, 1), float(hf), np.float32),
        )
        in_maps.append(m)
    return in_maps


def _assemble(res):
    y_prompt = np.empty((4, SEQ, D), np.float32)
    y_sample = np.empty((128, 4, D), np.float32)
    conv_p = np.empty((1, 4, 3, LRUW), np.float32)
    lru_p = np.empty((1, 4, LRUW), np.float32)
    k_p = np.empty((1, 4, 128, 2, 64), np.float32)
    v_p = np.empty((1, 4, 128, 2, 64), np.float32)
    conv_s = np.empty((1, 128, 3, LRUW), np.float32)
    lru_s = np.empty((1, 128, LRUW), np.float32)
    k_s = np.empty((1, 128, 128, 2, 64), np.float32)
    v_s = np.empty((1, 128, 128, 2, 64), np.float32)
    for c in range(NCORES):
        r = res[c]
        b, hf = c // 2, c % 2
        y_prompt[b, hf * HALF:(hf + 1) * HALF] = r["yp"]
        bs = slice(c * NB, (c + 1) * NB)
        y_sample[bs] = r["ys"].reshape(4, NB, D).transpose(1, 0, 2)
        conv_s[0, bs] = r["convs"].reshape(3, NB, LRUW).transpose(1, 0, 2)
        lru_s[0, bs] = r["lrus"]
        k_s[0, bs] = r["ks"].reshape(NB, 128, 2, 64)
        v_s[0, bs] = r["vs"].reshape(NB, 128, 2, 64)
        if hf == 1:
            conv_p[0, b] = r["convp"]
            lru_p[0, b] = r["lrup"][0]
            k_p[0, b] = r["kp"].reshape(128, 2, 64)
            v_p[0, b] = r["vp"].reshape(128, 2, 64)
    return (y_prompt, y_sample, conv_p, lru_p, k_p, v_p, conv_s, lru_s, k_s, v_s)


def kernel(**inputs):
    nc = _get_nc()
    in_maps = _prep_inputs(inputs)
    res = run_bass_kernel_spmd(nc, in_maps, core_ids=list(range(NCORES)))
    return _assemble(res.results)
```

```python
import numpy as np
from contextlib import ExitStack
import concourse.bass as bass
import concourse.mybir as mybir
from concourse.bass_utils import run_bass_kernel_spmd

F32 = mybir.dt.float32
BF16 = mybir.dt.bfloat16
AF = mybir.ActivationFunctionType
ALU = mybir.AluOpType

NCORES = 8
D = 1024
SEQ = 8192
HALF = 4096
NB = 16
NS = 64
LRUW = 512
DFF = 2816
NF = 22
ALPHA = float(2.0 ** 0.25)
SCALE = 0.125
EPS_LN = 1e-5
EPS_RMS = 1e-6
PAST = 16384
NV = 768 + 512 + 5 * 1024 + 8
OFF_BQKV, OFF_GATT, OFF_BOUT, OFF_L1G, OFF_L1B, OFF_L2G, OFF_L2B, OFF_SINK = (
    0, 768, 1280, 2304, 3328, 4352, 5376, 6400)
C_BXR, C_BGATE, C_CONVW, C_CONVB, C_BA, C_BX, C_LAM, C_GLRU = 0, 4, 8, 24, 28, 32, 36, 40
NBF = 44


class Buf:
    __slots__ = ("name", "w", "r")

    def __init__(self, name):
        self.name = name
        self.w = None
        self.r = {}


class Prog:
    ENG = ("pe", "act", "dve", "pool", "sp")

    def __init__(self, nc, es):
        self.nc = nc
        self.es = es
        self.q = {e: [] for e in self.ENG}
        self.sem = {}
        self.cnt = {}
        self.known = {e: {} for e in self.ENG}
        for e in self.ENG:
            self.sem[e] = es.enter_context(nc.semaphore("s_" + e))
            self.cnt[e] = 0
        self.nbufs = 0
        self.rec = None

    def record(self, fn):
        assert self.rec is None
        self.rec = []
        try:
            fn()
            return self.rec
        finally:
            self.rec = None

    DUR = {"pe": 0.33, "act": 0.68, "dve": 0.65, "pool": 1.25, "sp": 0.1}

    def merge(self, lists, skew=False, offs=None):
        n = len(lists)
        idx = [0] * n
        tfree = {e: 0.0 for e in self.ENG}
        ready = {}
        lastrd = {}
        tset = [None]
        LAT = 0.12
        remaining = sum(len(l) for l in lists)

        def est(it):
            if it[0] == "op":
                eng, reads, writes, meta = it[1], it[3], it[4], it[5]
            else:
                eng, reads, writes, meta = it[1], it[4], it[5], None
            t = tfree[eng]
            for b_ in reads:
                t = max(t, ready.get(id(b_), 0.0) + LAT)
            for b_ in writes:
                t = max(t, ready.get(id(b_), 0.0) + LAT, lastrd.get(id(b_), 0.0) + LAT)
            dur = self.DUR[eng] if it[0] == "op" else 2.0
            kind = None
            if meta is not None:
                kind, dur = meta
            pen = 0.0
            if kind in ("te", "sq") and tset[0] not in (None, kind):
                pen = 2.6
            return t + pen, dur + pen, eng, reads, writes, kind

        while remaining > 0:
            best = None
            for i, l in enumerate(lists):
                if idx[i] < len(l):
                    e_ = est(l[idx[i]])
                    key = (e_[0], i)
                    if best is None or key < best[0]:
                        best = (key, i, e_)
            _, i, (t0, dur, eng, reads, writes, meta) = best
            it = lists[i][idx[i]]
            idx[i] += 1
            remaining -= 1
            t1 = t0 + dur
            if it[0] == "op":
                tfree[eng] = t1
                if meta in ("te", "sq"):
                    tset[0] = meta
                self.op(it[1], it[2], it[3], it[4])
            else:
                tfree[eng] = t0 + 0.1
                self.dma(it[1], it[2], it[3], it[4], it[5], it[6])
            for b_ in reads:
                lastrd[id(b_)] = max(lastrd.get(id(b_), 0.0), t1)
            for b_ in writes:
                ready[id(b_)] = t1

    def buf(self, name=None):
        self.nbufs += 1
        return Buf(name or ("b%d" % self.nbufs))

    def bufs(self, n, name="b"):
        return [self.buf("%s%d" % (name, i)) for i in range(n)]

    def _wait(self, eng, k, v):
        if k == eng and eng in ("pe", "sp"):
            return
        if self.known[eng].get(k, 0) < v:
            self.known[eng][k] = v
            self.q[eng].append(("wait", k, v))

    def _deps(self, eng, reads, writes):
        deps = {}

        def add(k, v):
            if deps.get(k, 0) < v:
                deps[k] = v

        for b in reads:
            if b.w is not None:
                add(*b.w)
        for b in writes:
            if b.w is not None:
                add(*b.w)
            for k, v in b.r.items():
                add(k, v)
        for k, v in deps.items():
            self._wait(eng, k, v)

    def _mark(self, t, reads, writes):
        for b in reads:
            if b.r.get(t[0], 0) < t[1]:
                b.r[t[0]] = t[1]
        for b in writes:
            b.w = t
            b.r = {}

    def op(self, eng, fn, reads=(), writes=(), meta=None):
        if self.rec is not None:
            self.rec.append(("op", eng, fn, list(reads), list(writes), meta))
            return None
        self._deps(eng, reads, writes)
        self.cnt[eng] += 1
        t = (eng, self.cnt[eng])
        self.q[eng].append(("op", fn))
        self._mark(t, reads, writes)
        return t

    def dma(self, qeng, out, in_, reads=(), writes=(), chan=None):
        if self.rec is not None:
            self.rec.append(("dma", qeng, out, in_, list(reads), list(writes), chan))
            return None
        if chan not in self.sem:
            self.sem[chan] = self.es.enter_context(self.nc.semaphore("d_" + chan))
            self.cnt[chan] = 0
        self._deps(qeng, reads, writes)
        if self.cnt[chan] > 0:
            self._wait(qeng, chan, self.cnt[chan])
        self.cnt[chan] += 16
        t = (chan, self.cnt[chan])
        self.q[qeng].append(("dma", out, in_, chan))
        self._mark(t, reads, writes)
        return t

    def fence(self, bufs, engines=("pe", "act", "dve")):
        for eng in engines:
            self._deps(eng, (), bufs)

    @staticmethod
    def _n(ap):
        try:
            v = ap.free_size
            return int(v() if callable(v) else v)
        except Exception:
            return 512

    def mm(self, out, lhsT, rhs, start, stop, reads, writes):
        d = 0.06 + self._n(out) * 0.00075 * (4 if lhsT.dtype == F32 else 1)
        return self.op("pe", lambda e: e.matmul(out, lhsT, rhs, start=start, stop=stop), reads, writes, meta=(None, d))

    def tr(self, out, in_, ident, reads, writes):
        return self.op("pe", lambda e: e.transpose(out, in_, ident), reads, writes, meta=(None, 0.2))

    def act(self, out, in_, func, reads, writes, bias=None, scale=None, accum_out=None):
        kw = {}
        if bias is not None:
            kw["bias"] = bias
        if scale is not None:
            kw["scale"] = scale
        if accum_out is not None:
            kw["accum_out"] = accum_out
        kind = "sq" if func == AF.Sqrt else ("te" if func in (AF.Tanh, AF.Exp) else None)
        return self.op("act", lambda e: e.activation(out, in_, func, **kw), reads, writes,
                       meta=(kind, 0.22 + self._n(out) * 0.00075))

    def _dur(self, eng, out, two_src):
        n = self._n(out)
        if eng == "pool":
            return 0.2 + n * 0.0021
        if eng == "act":
            return 0.22 + n * 0.00075
        return 0.07 + n * (0.0021 if two_src else 0.00105)

    def cp(self, eng, out, in_, reads, writes):
        m = (None, self._dur(eng, out, False))
        if eng == "act":
            return self.op("act", lambda e: e.copy(out, in_), reads, writes, meta=m)
        return self.op(eng, lambda e: e.tensor_copy(out, in_), reads, writes, meta=m)

    def tt(self, eng, out, in0, in1, op, reads, writes):
        two = not (str(in0.space).upper().find("PSUM") >= 0 or str(in1.space).upper().find("PSUM") >= 0)
        return self.op(eng, lambda e: e.tensor_tensor(out, in0, in1, op), reads, writes, meta=(None, self._dur(eng, out, two)))

    def ts(self, eng, out, in0, s1, s2, op0, op1, reads, writes):
        m = (None, self._dur(eng, out, False))
        if op1 is None:
            return self.op(eng, lambda e: e.tensor_scalar(out, in0, s1, None, op0), reads, writes, meta=m)
        return self.op(eng, lambda e: e.tensor_scalar(out, in0, s1, s2, op0, op1), reads, writes, meta=m)

    def stt(self, out, in0, scalar, in1, op0, op1, reads, writes, accum_out=None):
        two = not (str(in0.space).upper().find("PSUM") >= 0 or str(in1.space).upper().find("PSUM") >= 0)
        m = (None, self._dur("dve", out, two))
        if accum_out is not None:
            return self.op("dve", lambda e: e.scalar_tensor_tensor(out, in0, scalar, in1, op0, op1, accum_out=accum_out),
                           reads, writes, meta=m)
        return self.op("dve", lambda e: e.scalar_tensor_tensor(out, in0, scalar, in1, op0, op1), reads, writes, meta=m)

    def memset(self, eng, ap, val, writes):
        return self.op(eng, lambda e: e.memset(ap, val), (), writes)

    def emit(self):
        nc = self.nc
        for k, v in self.cnt.items():
            if k not in self.ENG and v > 0:
                self._wait("sp", k, v)
        block = self.es.enter_context(nc.Block())
        sem = self.sem

        def replay(name, e):
            own = sem[name]
            for it in self.q[name]:
                if it[0] == "wait":
                    e.wait_ge(sem[it[1]], it[2])
                elif it[0] == "op":
                    it[1](e).then_inc(own, 1)
                else:
                    e.dma_start(out=it[1], in_=it[2]).then_inc(sem[it[3]], 16)

        @block.sync
        def _(e):
            replay("sp", e)

        @block.gpsimd
        def _(e):
            replay("pool", e)

        @block.tensor
        def _(e):
            replay("pe", e)

        @block.scalar
        def _(e):
            replay("act", e)

        @block.vector
        def _(e):
            replay("dve", e)


def build_program(n_pre=8, n_main=8, do_sample=True, dbg=False, stage=99, PREFIX_OFFS=(0, 0, 0.5, 0.5)):
    nc = bass.Bass("TRN2", target_bir_lowering=False)
    es = ExitStack()

    def din(name, shape, dt=F32):
        return nc.dram_tensor(name, list(shape), dt, kind="ExternalInput").ap()

    def dout(name, shape, dt=F32):
        return nc.dram_tensor(name, list(shape), dt, kind="ExternalOutput").ap()

    xpre = din("xpre", [HALF, D])
    xmain = din("xmain", [HALF, D])
    xs = din("xs", [NS, D])
    ck = din("ck", [NB, 128, 128])
    cv = din("cv", [NB, 128, 128])
    sconv = din("sconv", [48, LRUW])
    slru = din("slru", [NB, LRUW])
    w_in = din("w_in", [D, 1792])
    w_out = din("w_out", [D, D])
    w_gate = din("w_gate", [D, DFF])
    w_up = din("w_up", [D, DFF])
    w_down = din("w_down", [DFF, D])
    wab_d = din("wab", [128, 4, 128])
    wxb_d = din("wxb", [128, 4, 128])
    bfm_d = din("bfm", [128, NBF])
    vrow_d = din("vrow", [1, NV])
    ropem_d = din("ropem", [8, 128, 4, 96])
    ropex_d = din("ropex", [128, 96])
    ropes_d = din("ropes", [NS, 96])
    maskp_d = din("maskp", [128, 2, 128])
    masks_d = din("masks", [128, 17, 64])
    ident_d = din("ident", [128, 128])
    flag_d = din("flag", [128, 1])

    yp = dout("yp", [HALF, D])
    ys = dout("ys", [NS, D])
    convp = dout("convp", [3, LRUW])
    lrup = dout("lrup", [1, LRUW])
    kp = dout("kp", [128, 128])
    vp = dout("vp", [128, 128])
    convs = dout("convs", [48, LRUW])
    lrus = dout("lrus", [NB, LRUW])
    ks = dout("ks", [NB, 128, 128])
    vs = dout("vs", [NB, 128, 128])

    wg_b = nc.dram_tensor("wg_b", [D, DFF], BF16, kind="Internal").ap()
    wu_b = nc.dram_tensor("wu_b", [D, DFF], BF16, kind="Internal").ap()
    wd_b = nc.dram_tensor("wd_b", [DFF, D], BF16, kind="Internal").ap()

    p = Prog(nc, es)

    class Cut(Exception):
        pass

    cks = []

    def ckp(name):
        cks.append(name)
        if dbg and len(cks) == dbg:
            print("CUT at checkpoint", len(cks), name)
            raise Cut()

    def sb(name, shape, dt=F32):
        return es.enter_context(nc.sbuf_tensor(name, list(shape), dt))

    w_in_sb = sb("w_in_sb", [128, 8, 1792], BF16)
    w_out_sb = sb("w_out_sb", [128, 8, 1024], BF16)
    wab_sb = sb("wab_sb", [128, 4, 128], BF16)
    wxb_sb = sb("wxb_sb", [128, 4, 128], BF16)
    vrow_sb = sb("vrow_sb", [128, NV])
    bfm_sb = sb("bfm_sb", [128, NBF])
    sc_sb = sb("sc_sb", [128, 8])
    hb_sb = sb("hb_sb", [128, 8])
    esink_sb = sb("esink_sb", [128, 8])
    ident_f = sb("ident_f", [128, 128])
    ident_b = sb("ident_b", [128, 128], BF16)
    ones_f = sb("ones_f", [128, 8])
    zero_b = sb("zero_b", [128, 272], BF16)
    zero_f = zero_b[:, 0:256].bitcast(F32)
    maskp_sb = sb("maskp_sb", [128, 2, 128], BF16)
    masks_sb = sb("masks_sb", [128, 17, 64], BF16)
    flag_sb = sb("flag_sb", [128, 1])
    xtok = sb("xtok", [128, 4, 1024])
    rope_sb = sb("rope_sb", [128, 4, 96])
    h1 = sb("h1", [128, 4, 1024])
    mixT = sb("mixT", [128, 8, 512], BF16)
    kT = sb("kT", [128, 4, 640], BF16)
    vaug = sb("vaug", [128, 5, 2, 65], BF16)
    hist = sb("hist", [128, 4, 48])
    hcar = sb("hcar", [128, 4, 16])
    wgu = sb("wgu", [128, 3, 2, 8, 128], BF16)
    wdn = sb("wdn", [128, 3, 2, 512], BF16)
    hidT = sb("hidT", [128, NF, 512], BF16)
    h1T = sb("h1T", [128, 8, 512], BF16)
    xr_ext = sb("xr_ext", [128, 2, 516])
    gg = sb("gg", [128, 2, 512])
    hh = sb("hh", [128, 2, 512])
    NTMP = 4
    tmp = sb("tmp", [128, NTMP, 1024])
    qrot = sb("qrot", [128, 512], BF16)
    kz = sb("kz", [128, 512], BF16)
    yn = sb("yn", [128, 2, 512], BF16)
    xcb = sb("xcb", [128, 2, 512], BF16)
    st = sb("st", [128, 192])
    bnst = sb("bnst", [128, 4, 2, 6])
    k32 = sb("k32", [128, 128])

    psf = [es.enter_context(nc.psum_tensor("psf%d" % i, [128, 512], F32)) for i in range(6)]
    psbs = [es.enter_context(nc.psum_tensor("psb%d" % i, [128, 1024], BF16)) for i in range(2)]
    B_psf = p.bufs(6, "psf")
    B_psb = p.bufs(2, "psb")
    ps_rr = [0]
    psb_rr = [0]

    def psum():
        i = ps_rr[0] % 4
        ps_rr[0] += 1
        return psf[i], B_psf[i]

    ps_rr6 = [0]

    def psum6():
        i = ps_rr6[0] % 6
        ps_rr6[0] += 1
        return psf[i], B_psf[i]

    def psumb():
        i = psb_rr[0] % 2
        psb_rr[0] += 1
        return psbs[i][:, 0:512], B_psb[i]

    tmp_rr = [0]
    B_tmp = p.bufs(NTMP, "tmp")

    class ChainRes:
        def __init__(self, par, banks=None, tmps=None, psb=None, idx=None):
            self.par = par
            self.idx = par if idx is None else idx
            self.banks = banks if banks is not None else [2 * par, 2 * par + 1]
            self.n = 0
            self.sto = 40 * self.idx
            self.tmps = tmps if tmps is not None else [(tmp[:, 2 * par + i, :], B_tmp[2 * par + i]) for i in range(2)]
            self.psb = psb if psb is not None else [par, par]

        def psum(self):
            i = self.banks[self.n % len(self.banks)]
            self.n += 1
            return psf[i], B_psf[i]

        def tmp(self, k):
            return self.tmps[k]

        def psumb(self, k):
            i = self.psb[k]
            return psbs[i][:, 0:512], B_psb[i]

        slots = None

        def slot(self, name):
            if self.slots is not None:
                return self.slots[name]
            return {"xe": (xr_ext[:, self.par, :], B_xr[self.par]), "gg": (gg[:, self.par, :], B_gg[self.par]),
                    "hh": (hh[:, self.par, :], B_hh[self.par]), "xcb": (xcb[:, self.par, :], B_xcb[self.par])}[name]

    CH = [ChainRes(0), ChainRes(1)]
    B_tq = p.bufs(2, "tq")
    tq = h1T[:, :, :].rearrange("p a b -> p (a b)").bitcast(F32).rearrange("p (i x) -> p i x", i=2)
    CH_LA = ChainRes(0, banks=[0])
    CH_LB = ChainRes(1, banks=[1])
    CH_Q = ChainRes(0, banks=[2, 3], tmps=[(tq[:, 0, :], B_tq[0]), (tq[:, 1, :], B_tq[1])], psb=[0, 1])
    CH_S = ChainRes(0, psb=[0, 1])
    CH4 = [ChainRes(i % 2, banks=[i], idx=i) for i in range(4)]

    def gettmp():
        i = tmp_rr[0] % NTMP
        tmp_rr[0] += 1
        return tmp[:, i, :], B_tmp[i]

    B_win = p.bufs(8, "win")
    B_wout = p.bufs(2, "wout")
    B_const = p.buf("const")
    B_xtok = p.bufs(4, "xtok")
    B_rope = p.buf("rope")
    B_h1 = p.bufs(4, "h1")
    B_mixT = [[p.buf() for _ in range(4)] for _ in range(8)]
    B_kT = p.bufs(5, "kT")
    B_vaug = p.bufs(5, "vaug")
    B_hist = p.bufs(4, "hist")
    B_hcar = p.bufs(4, "hcar")
    B_wg = p.bufs(3, "wg")
    B_wu = p.bufs(3, "wu")
    B_wdn = p.bufs(3, "wdn")
    B_hid = p.bufs(NF, "hid")
    B_h1T = p.bufs(4, "h1T")
    B_xT = p.bufs(4, "xT")
    B_qT = p.bufs(4, "qT")
    B_pT = p.bufs(4, "pT")
    B_kTc = p.bufs(2, "kTc")
    B_vaugc = p.buf("vaugc")
    B_kcd = p.buf("kcd")
    B_xr = p.bufs(2, "xr")
    B_gg = p.bufs(2, "gg")
    B_hh = p.bufs(2, "hh")
    B_xcb = p.bufs(2, "xcb")
    B_qrot = p.buf("qrot")
    B_kdup = p.buf("kdup")
    B_yn = p.bufs(2, "yn")
    B_st2 = p.bufs(4, "st")
    B_st = B_st2[0]
    B_bnst = p.bufs(4, "bnst")
    B_k32 = p.buf("k32")
    B_wgb = p.buf("wgb")
    B_wub = p.buf("wub")
    B_wdb = p.buf("wdb")
    B_rstd = p.buf("rstd")
    B_rstda = p.bufs(4, "rstda")

    xT = hidT[:, 0:8, :]
    qT = hidT[:, 8:12, :]
    pT_all = hidT[:, 12:20, :]
    ARENA_A = B_xT + B_qT + B_pT + B_kTc + [B_kcd]
    ARENA_B = B_hid + B_h1T

    rstd_sb = sb("rstd_sb", [128, 16])

    B_px = p.bufs(12, "px")
    hq = hidT[:, 8:16, :].rearrange("p a b -> p (a b)").bitcast(F32).rearrange("p (i x) -> p i x", i=2)
    xe2 = hidT[:, 16:21, :].rearrange("p a b -> p (a b)").bitcast(F32)[:, 0:1032].rearrange("p (i x) -> p i x", i=2)
    CHP = []
    _ptmps = [[(tmp[:, 0, :], B_tmp[0]), (tmp[:, 1, :], B_tmp[1])], [(tmp[:, 2, :], B_tmp[2]), (tmp[:, 3, :], B_tmp[3])],
              [(tq[:, 0, :], B_tq[0]), (tq[:, 1, :], B_tq[1])], [(hq[:, 0, :], B_px[0]), (hq[:, 1, :], B_px[1])]]
    for i in range(4):
        r = ChainRes(i % 2, banks=[i], tmps=_ptmps[i], idx=i)
        if i >= 2:
            r.slots = {"xe": (xe2[:, i - 2, :], B_px[2 + i]), "gg": (None, None),
                       "hh": (gg[:, i - 2, :], B_gg[i - 2]), "xcb": (mixT[:, i - 2, :], B_px[6 + i])}
        CHP.append(r)
    PREFIX_ALIAS = B_px + B_tq

    B_c = {n: p.buf("c_" + n) for n in ("ident", "bfm", "vrow", "flag", "identb", "wab", "wxb", "maskp", "masks", "misc", "sc")}
    p.dma("sp", ident_f[:], ident_d, writes=[B_c["ident"]], chan="c0")
    p.dma("sp", bfm_sb[:], bfm_d, writes=[B_c["bfm"]], chan="c1")
    B_winA = p.bufs(8, "winA")
    for k in range(8):
        p.dma("pool", w_in_sb[:, k, 0:512], w_in[k * 128:(k + 1) * 128, 0:512], writes=[B_winA[k]], chan="g%d" % (1 + k % 4))
    for k in range(8):
        p.dma("pool", w_in_sb[:, k, 512:1792], w_in[k * 128:(k + 1) * 128, 512:1792], writes=[B_win[k]], chan="g%d" % (1 + k % 4))
    p.dma("pool", ident_b[:], ident_d, writes=[B_c["identb"]], chan="g0")
    p.dma("pool", wab_sb[:], wab_d, writes=[B_c["wab"]], chan="g5")
    p.dma("pool", wxb_sb[:], wxb_d, writes=[B_c["wxb"]], chan="g6")
    p.dma("sp", vrow_sb[:], vrow_d[0].partition_broadcast(128), writes=[B_c["vrow"]], chan="c2")
    p.dma("sp", flag_sb[:], flag_d, writes=[B_c["flag"]], chan="c3")
    p.dma("pool", maskp_sb[:], maskp_d, writes=[B_c["maskp"]], chan="g7")
    p.dma("pool", masks_sb[:], masks_d, writes=[B_c["masks"]], chan="g8")
    for hlf in range(2):
        p.dma("pool", w_out_sb[:, 4 * hlf:4 * hlf + 4, :],
              w_out[hlf * 512:(hlf + 1) * 512, :].rearrange("(k p) n -> p k n", p=128),
              writes=[B_wout[hlf]], chan="g%d" % (9 + hlf))
    p.dma("pool", wg_b, w_gate, reads=B_win, writes=[B_wgb], chan="g11")
    p.dma("pool", wu_b, w_up, reads=B_win, writes=[B_wub], chan="g12")
    p.dma("pool", wd_b, w_down, reads=B_win, writes=[B_wdb], chan="g13")

    p.memset("dve", ones_f[:], 1.0, [B_c["misc"]])
    p.memset("dve", zero_b[:], 0.0, [B_c["misc"]])
    p.memset("dve", vaug[:, :, :, 64:65], 1.0, B_vaug)
    p.memset("dve", hist[:], 0.0, B_hist)
    p.memset("dve", hcar[:], 0.0, B_hcar)
    p.memset("dve", kT[:], 0.0, B_kT)
    p.memset("dve", kz[:], 0.0, [B_kdup])
    p.act(st[:, 0:4], bfm_sb[:, C_LAM:C_LAM + 4], AF.Exp, [B_c["bfm"]], [B_st], scale=-1.0)
    p.act(st[:, 4:8], st[:, 0:4], AF.Ln, [B_st], [B_st], bias=1.0)
    p.ts("dve", sc_sb[:, 0:4], st[:, 4:8], -8.0, None, ALU.mult, None, [B_st], [B_c["sc"]])
    p.ts("dve", sc_sb[:, 4:8], st[:, 4:8], -4.0, None, ALU.mult, None, [B_st], [B_c["sc"]])
    p.ts("dve", hb_sb[:, 0:8], bfm_sb[:, C_BA:C_BA + 8], 0.5, None, ALU.mult, None, [B_c["bfm"]], [B_c["sc"]])
    p.act(esink_sb[:], vrow_sb[:, OFF_SINK:OFF_SINK + 8], AF.Exp, [B_c["vrow"]], [B_c["sc"]])

    CST = list(B_c.values())
    LN_ENG = "pool"

    def load_x(src_rows, nsub, nt, rope_src=None):
        for s in range(nsub):
            p.dma("sp", xtok[0:nt, s, :], src_rows(s), writes=[B_xtok[s]], chan="x%d" % s)
        if rope_src is not None:
            p.dma("sp", rope_src[0], rope_src[1], writes=[B_rope], chan="rope")

    XS = [xT, [[b_] for b_ in B_xT]]

    def transposes_x(nsub, nt, dst_xs=None, bank_fn=None, extra_w=()):
        xTv, BxTv = dst_xs if dst_xs is not None else (XS[0], XS[1])
        for s in range(nsub):
            for g in range(2):
                bank, Bb = (bank_fn or psum)()
                for j in range(4):
                    kc = g * 4 + j
                    p.tr(bank[:, j * 128:j * 128 + nt], xtok[0:nt, s, kc * 128:(kc + 1) * 128], ident_f[0:nt, 0:nt],
                         [B_xtok[s]] + CST, [Bb])
                src = bank[:].rearrange("p (a b) -> p a b", b=128)[:, :, 0:nt]
                dst = xTv[:, g * 4:(g + 1) * 4, s * nt:(s + 1) * nt]
                if g == 0:
                    p.cp("act", dst, src, [Bb], list(BxTv[s]) + list(extra_w))
                else:
                    p.cp("dve", dst, src, [Bb], list(BxTv[s]) + list(extra_w))

    def lru_chunk(c, ncol, nsub, nt, sample, with_gate, res):
        hw, tsz = (48, 16) if sample else (3, 1)
        slot = res.par
        xe, Bxe = res.slot("xe")
        gg_, Bgg = res.slot("gg")
        hh_, Bhh = res.slot("hh")
        xcb_, Bxcb = res.slot("xcb")
        xT = XS[0]
        xTr = [b_ for s_ in range(nsub) for b_ in XS[1][s_]]
        X, BX = res.tmp(0)
        Y, BY = res.tmp(1)
        p.cp("dve", xe[:, 0:hw], hist[:, c, 0:hw], [B_hist[c]], [Bxe])
        bank, Bb = res.psum()
        for k in range(8):
            p.mm(bank[:, 0:ncol], w_in_sb[:, k, c * 128:(c + 1) * 128], xT[:, k, 0:ncol], k == 0, k == 7,
                 [B_winA[k]] + xTr, [Bb])
        p.act(xe[:, hw:hw + ncol], bank[:, 0:ncol], AF.Identity, [Bb] + CST, [Bxe], bias=bfm_sb[:, C_BXR + c:C_BXR + c + 1])
        p.cp("dve", hist[:, c, 0:hw], xe[:, ncol:ncol + hw], [Bxe], [B_hist[c]])
        if with_gate:
            bank2, Bb2 = res.psum()
            for k in range(8):
                p.mm(bank2[:, 0:ncol], w_in_sb[:, k, 512 + c * 128:512 + (c + 1) * 128], xT[:, k, 0:ncol], k == 0, k == 7,
                     [B_win[k]] + xTr, [Bb2])
            gx_ = X[:, 0:ncol]
            gsl = gg_[:, 0:ncol]
            p.act(gsl, bank2[:, 0:ncol], AF.Identity, [Bb2] + CST, [Bgg],
                  bias=bfm_sb[:, C_BGATE + c:C_BGATE + c + 1])
            p.tt("dve", gx_, gsl, gsl, ALU.mult, [Bgg], [BX])
            p.ts("dve", gx_, gx_, 0.044715, 1.0, ALU.mult, ALU.add, [BX], [BX])
            p.tt("dve", gx_, gx_, gsl, ALU.mult, [BX, Bgg], [BX])
            p.act(gx_, gx_, AF.Tanh, [BX], [BX], scale=0.7978845608028654)
            p.stt(gsl, gx_, 1.0, gsl, ALU.add, ALU.mult, [BX, Bgg], [Bgg])
        xc = X[:, 512:512 + ncol]
        cw = lambda j: bfm_sb[:, C_CONVW + j * 4 + c:C_CONVW + j * 4 + c + 1]
        cacc, Bca = res.psum()
        ca = cacc[:, 0:ncol]
        p.ts("dve", ca, xe[:, 0:ncol], cw(0), bfm_sb[:, C_CONVB + c:C_CONVB + c + 1], ALU.mult, ALU.add,
             [Bxe] + CST, [Bca])
        for j in range(1, 3):
            p.stt(ca, xe[:, j * tsz:j * tsz + ncol], cw(j), ca, ALU.mult, ALU.add, [Bxe, Bca] + CST, [Bca])
        p.stt(xc, xe[:, 3 * tsz:3 * tsz + ncol], cw(3), ca, ALU.mult, ALU.add, [Bxe, Bca] + CST, [BX])
        xb = xcb_[:, 0:ncol]
        p.cp("act" if with_gate else "pool", xb, xc, [BX], [Bxcb])
        a = Y[:, 0:ncol]
        sq = X[:, 0:ncol]
        iu_ = Y[:, 512:512 + ncol]
        bR, BR = res.psum()
        p.mm(bR[:, 0:ncol], wab_sb[:, c, :], xb, True, True, [Bxcb] + CST, [BR])
        p.act(a, bR[:, 0:ncol], AF.Tanh, [BR] + CST, [BY], scale=0.5, bias=hb_sb[:, c:c + 1])
        bI, BI = res.psum()
        p.mm(bI[:, 0:ncol], wxb_sb[:, c, :], xb, True, True, [Bxcb] + CST, [BI])
        p.act(iu_, bI[:, 0:ncol], AF.Tanh, [BI] + CST, [BY], scale=0.5, bias=hb_sb[:, 4 + c:5 + c])
        p.act(sq, a, AF.Exp, [BY, BX] + CST, [BX], scale=sc_sb[:, c:c + 1], bias=sc_sb[:, c:c + 1])
        p.act(a, a, AF.Exp, [BY] + CST, [BY], scale=sc_sb[:, 4 + c:5 + c], bias=sc_sb[:, 4 + c:5 + c])
        p.act(sq, sq, AF.Relu, [BX], [BX], scale=-1.0, bias=1.0)
        p.act(sq, sq, AF.Sqrt, [BX], [BX])
        p.stt(iu_, iu_, 1.0, xc, ALU.add, ALU.mult, [BY, BX], [BY])
        p.stt(iu_, iu_, 0.5, sq, ALU.mult, ALU.mult, [BY, BX], [BY])
        hs = hh_[:, 0:ncol]
        if not sample:
            p.op("dve", lambda e: e.tensor_tensor_scan(hs, a, iu_, hcar[:, c, 0:1], ALU.mult, ALU.add),
                 [BY, B_hcar[c]], [Bhh])
            p.cp("dve", hcar[:, c, 0:1], hh_[:, ncol - 1:ncol], [Bhh], [B_hcar[c]])
        else:
            for t in range(4):
                prev = hcar[:, c, 0:16] if t == 0 else hh_[:, (t - 1) * 16:t * 16]
                cur = hh_[:, t * 16:(t + 1) * 16]
                p.tt("dve", cur, Y[:, t * 16:(t + 1) * 16], prev, ALU.mult, [BY, B_hcar[c], Bhh], [Bhh])
                p.tt("dve", cur, cur, Y[:, 512 + t * 16:512 + (t + 1) * 16], ALU.add, [BY, Bhh], [Bhh])
        return slot

    def lru_mix(c, slot, ncol, nsub, nt, ssq_bank, Bssq, res, last_chain=True):
        X, BX = res.tmp(0)
        gg_, Bgg = res.slot("gg")
        hh_, Bhh = res.slot("hh")
        yl = X[:, 512:512 + ncol]
        ysq = X[:, 0:ncol]
        p.stt(yl, gg_[:, 0:ncol], 0.5, hh_[:, 0:ncol], ALU.mult, ALU.mult, [Bgg, Bhh, BX], [BX])
        p.tt("dve", ysq, yl, yl, ALU.mult, [BX], [BX])
        p.act(mixT[:, c, 0:ncol], yl, AF.Identity, [BX] + CST, [B_mixT[c][s] for s in range(nsub)],
              scale=bfm_sb[:, C_GLRU + c:C_GLRU + c + 1])
        for s in range(nsub):
            p.mm(ssq_bank[0:nt, 2 * s:2 * s + 2], X[:, s * nt:(s + 1) * nt], ones_f[:, 0:2], False, False,
                 [BX] + CST, [Bssq])

    def qkv_sub(s, nt, vslot, kcol, res, q=True, want_k32=False, want_v32=None):
        xT = XS[0]
        xTs = list(XS[1][s])
        ta_, Bta = res.tmp(0)
        tb_, Btb = res.tmp(1)
        cosv = rope_sb[0:nt, s, 0:32]
        sinv = rope_sb[0:nt, s, 32:64]
        nsinv = rope_sb[0:nt, s, 64:96]
        RP = [B_rope]

        def rope(src, H, Bm):
            sv = src.rearrange("p (h t d) -> p h t d", t=2, d=32)
            Bv = Bm.rearrange("p (h t d) -> p h t d", t=2, d=32)
            cb = cosv.unsqueeze(1).unsqueeze(1).broadcast_to([nt, H, 2, 32])
            p.tt("dve", Bv[:, :, 0, :], sv[:, :, 1, :], nsinv.unsqueeze(1).broadcast_to([nt, H, 32]), ALU.mult,
                 [Bta] + RP, [Btb])
            p.tt("dve", Bv[:, :, 1, :], sv[:, :, 0, :], sinv.unsqueeze(1).broadcast_to([nt, H, 32]), ALU.mult,
                 [Bta] + RP, [Btb])
            p.tt("dve", sv, sv, cb, ALU.mult, [Bta] + RP, [Bta])

        if q:
            bq, Bbq = res.psum()
            for k in range(8):
                p.mm(bq[0:nt, :], xT[:, k, s * nt:(s + 1) * nt], w_in_sb[:, k, 1024:1536], k == 0, k == 7,
                     xTs + [B_win[k]], [Bbq])
            qb = ta_[0:nt, 0:512]
            p.tt("dve", qb, bq[0:nt, :], vrow_sb[0:nt, OFF_BQKV:OFF_BQKV + 512], ALU.add, [Bbq] + CST, [Bta])
        bkv, Bbkv = res.psum()
        for k in range(8):
            p.mm(bkv[0:nt, 0:256], xT[:, k, s * nt:(s + 1) * nt], w_in_sb[:, k, 1536:1792], k == 0, k == 7,
                 xTs + [B_win[k]], [Bbkv])
        kvb = ta_[0:nt, 512:768]
        p.tt("dve", kvb, bkv[0:nt, 0:256], vrow_sb[0:nt, OFF_BQKV + 512:OFF_BQKV + 768], ALU.add, [Bbkv] + CST, [Bta])
        if q:
            rope(qb, 8, tb_[0:nt, 0:512])
            p.tt("dve", qrot[0:nt, :], qb, tb_[0:nt, 0:512], ALU.add, [Bta, Btb], [B_qrot])
            pb, Bpb = res.psumb(0)
            for c in range(4):
                p.tr(pb[:, c * 128:c * 128 + nt], qrot[0:nt, c * 128:(c + 1) * 128], ident_b[0:nt, 0:nt],
                     [B_qrot] + CST, [Bpb])
            p.cp("act", qT[:, :, s * nt:(s + 1) * nt], pb.rearrange("p (a b) -> p a b", b=128)[:, :, 0:nt], [Bpb], [B_qT[s]])
        kA = ta_[0:nt, 512:640]
        kB = tb_[0:nt, 512:640]
        rope(kA, 2, kB)
        for hf_ in range(2):
            p.tt("dve", kz[0:nt, :].rearrange("p (j r) -> p j r", j=2)[:, :, hf_ * 192:hf_ * 192 + 64],
                 kA.rearrange("p (j d) -> p j d", d=64), kB.rearrange("p (j d) -> p j d", d=64),
                 ALU.add, [Bta, Btb], [B_kdup])
        if want_k32:
            p.tt("dve", k32[0:nt, :], kA, kB, ALU.add, [Bta, Btb], [B_k32])
        p.cp("act", vaug[0:nt, vslot, :, 0:64], ta_[0:nt, 640:768].rearrange("p (j d) -> p j d", d=64), [Bta], [B_vaug[vslot]])
        if want_v32 is not None:
            want_v32(ta_[0:nt, 640:768], Bta)
        pb, Bpb = res.psumb(1)
        for j in range(4):
            p.tr(pb[:, j * 128:j * 128 + nt], kz[0:nt, j * 128:(j + 1) * 128], ident_b[0:nt, 0:nt], [B_kdup] + CST, [Bpb])
        p.cp("dve", kT[:, :, kcol:kcol + nt], pb[:, 0:512].rearrange("p (a b) -> p a b", b=128)[:, :, 0:nt],
             [Bpb], [B_kT[s + 1]])

    def attn_tail(s, nt, ob, res):
        par = res.par
        so = res.sto
        Bst = B_st2[res.idx]
        yat_, Byat = res.tmp(0)
        yat = yat_[0:nt, 0:512]
        for j in range(2):
            ov = ob[j][0][0:nt, 0:260].rearrange("p (g d) -> p g d", d=65)
            p.tt("dve", st[0:nt, so + 8 + 4 * j:so + 12 + 4 * j], ov[:, :, 64], esink_sb[0:nt, 4 * j:4 * j + 4], ALU.add,
                 [ob[j][1]] + CST, [Bst])
        p.op("dve", lambda e: e.reciprocal(st[0:nt, so + 16:so + 24], st[0:nt, so + 8:so + 16]), [Bst], [Bst])
        for j in range(2):
            ov = ob[j][0][0:nt, 0:260].rearrange("p (g d) -> p g d", d=65)
            p.tt("dve", yat[:, 256 * j:256 * (j + 1)].rearrange("p (g d) -> p g d", d=64), ov[:, :, 0:64],
                 st[0:nt, so + 16 + 4 * j:so + 20 + 4 * j].unsqueeze(2).broadcast_to([nt, 4, 64]), ALU.mult,
                 [ob[j][1], Bst], [Byat])
        p.stt(yat_[0:nt, 512:1024], yat, 1.0, yat, ALU.mult, ALU.mult, [Byat], [Byat, Bst], accum_out=st[0:nt, so + 24:so + 25])
        p.ts("dve", st[0:nt, so + 25:so + 26], st[0:nt, so + 24:so + 25], 1.0 / 512, EPS_RMS, ALU.mult, ALU.add, [Bst], [Bst])
        p.act(st[0:nt, so + 26:so + 27], st[0:nt, so + 25:so + 26], AF.Sqrt, [Bst], [Bst])
        p.op("dve", lambda e: e.reciprocal(rstd_sb[0:nt, 4 + s:5 + s], st[0:nt, so + 26:so + 27]), [Bst], [B_rstda[s]])
        ynp = yn[0:nt, par, :]
        p.tt("dve", ynp, yat, vrow_sb[0:nt, OFF_GATT:OFF_GATT + 512], ALU.mult, [Byat] + CST, [B_yn[par]])
        pb, Bpb = res.psumb(0)
        for c in range(4):
            p.tr(pb[:, c * 128:c * 128 + nt], yn[0:nt, par, c * 128:(c + 1) * 128], ident_b[0:nt, 0:nt], [B_yn[par]] + CST, [Bpb])
        p.cp("act", mixT[:, 4:8, s * nt:(s + 1) * nt], pb.rearrange("p (a b) -> p a b", b=128)[:, :, 0:nt],
             [Bpb], [B_mixT[4 + c][s] for c in range(4)])

    def attention_j(s, keyblocks, res):
        nt = 128
        par = res.par
        so = res.sto
        Bst = B_st2[res.idx]
        stb = (0, 1) if par == 0 else (2, 3)
        obi = 4 + par
        yat_, Byat = res.tmp(0)
        yat = yat_[0:nt, 0:512]
        nkb = len(keyblocks)
        it = 0
        for j in range(2):
            ob, Bob = psf[obi], B_psf[obi]
            p.mm(ob[0:nt, 0:260], zero_b[:, 0:nt], zero_b[:, 0:260], True, False, CST, [Bob])
            for ib, (nk, kTfn, vfn, mask, rds) in enumerate(keyblocks):
                ring = 2 * par + (it % 2)
                bank, Bb = psf[stb[it % 2]], B_psf[stb[it % 2]]
                it += 1
                pT = pT_all[:, 2 * ring, :].rearrange("p (h q) -> p h q", q=128)
                BpT = B_pT[ring]
                for g in range(4):
                    h = 4 * j + g
                    c, hf = h // 2, h % 2
                    p.mm(bank[0:nk, g * nt:(g + 1) * nt], kTfn(j, hf), qT[:, c, s * nt:(s + 1) * nt],
                         True, True, rds + [B_qT[s]], [Bb])
                pv = pT[0:nk, :, :]
                p.act(pv, bank[0:nk, 0:4 * nt].rearrange("p (g q) -> p g q", q=nt), AF.Exp, [Bb], [BpT], scale=SCALE)
                p.tt("dve", pv, pv, mask.unsqueeze(1).broadcast_to([nk, 4, nt]), ALU.mult, [BpT] + CST, [BpT])
                for g in range(4):
                    p.mm(ob[0:nt, g * 65:(g + 1) * 65], pT[0:nk, g, :], vfn(j), False, ib == nkb - 1 and g == 3,
                         rds + [BpT], [Bob])
            ov = ob[0:nt, 0:260].rearrange("p (g d) -> p g d", d=65)
            p.tt("dve", st[0:nt, so + 8:so + 12], ov[:, :, 64], esink_sb[0:nt, 4 * j:4 * j + 4], ALU.add, [Bob] + CST, [Bst])
            p.op("dve", lambda e: e.reciprocal(st[0:nt, so + 16:so + 20], st[0:nt, so + 8:so + 12]), [Bst], [Bst])
            p.tt("dve", yat[:, 256 * j:256 * (j + 1)].rearrange("p (g d) -> p g d", d=64), ov[:, :, 0:64],
                 st[0:nt, so + 16:so + 20].unsqueeze(2).broadcast_to([nt, 4, 64]), ALU.mult, [Bob, Bst], [Byat])
        p.stt(yat_[0:nt, 512:1024], yat, 1.0, yat, ALU.mult, ALU.mult, [Byat], [Byat, Bst], accum_out=st[0:nt, so + 24:so + 25])
        p.ts("dve", st[0:nt, so + 25:so + 26], st[0:nt, so + 24:so + 25], 1.0 / 512, EPS_RMS, ALU.mult, ALU.add, [Bst], [Bst])
        p.act(st[0:nt, so + 26:so + 27], st[0:nt, so + 25:so + 26], AF.Sqrt, [Bst], [Bst])
        p.op("dve", lambda e: e.reciprocal(rstd_sb[0:nt, 4 + s:5 + s], st[0:nt, so + 26:so + 27]), [Bst], [B_rstda[s]])
        ynp = yn[0:nt, par, :]
        p.tt("dve", ynp, yat, vrow_sb[0:nt, OFF_GATT:OFF_GATT + 512], ALU.mult, [Byat] + CST, [B_yn[par]])
        pb, Bpb = res.psumb(0)
        for c in range(4):
            p.tr(pb[:, c * 128:c * 128 + nt], yn[0:nt, par, c * 128:(c + 1) * 128], ident_b[0:nt, 0:nt], [B_yn[par]] + CST, [Bpb])
        p.cp("act", mixT[:, 4:8, s * nt:(s + 1) * nt], pb.rearrange("p (a b) -> p a b", b=128)[:, :, 0:nt],
             [Bpb], [B_mixT[4 + c][s] for c in range(4)])

    def layernorm(buf_ap, Bbuf, nt, og, ob_, res, bias_eng=None):
        par = res.par
        so = res.sto
        Bst = B_st2[res.idx]
        for hlf in range(2):
            p.op("dve", lambda e, hlf=hlf: e.bn_stats(bnst[0:nt, res.idx, hlf, :], buf_ap[0:nt, hlf * 512:(hlf + 1) * 512]),
                 [Bbuf], [B_bnst[res.idx]])
        p.op("dve", lambda e: e.bn_aggr(st[0:nt, so + 28:so + 30], bnst[0:nt, res.idx, :, :].rearrange("p a b -> p (a b)")),
             [B_bnst[res.idx]], [Bst])
        p.ts("dve", st[0:nt, so + 30:so + 31], st[0:nt, so + 29:so + 30], EPS_LN, None, ALU.add, None, [Bst], [Bst])
        p.act(st[0:nt, so + 31:so + 32], st[0:nt, so + 30:so + 31], AF.Sqrt, [Bst], [Bst])
        p.op("dve", lambda e: e.reciprocal(st[0:nt, so + 32:so + 33], st[0:nt, so + 31:so + 32]), [Bst], [Bst])
        p.stt(st[0:nt, so + 33:so + 34], st[0:nt, so + 28:so + 29], -1.0, st[0:nt, so + 32:so + 33], ALU.mult, ALU.mult, [Bst], [Bst])
        p.act(buf_ap[0:nt, :], buf_ap[0:nt, :], AF.Identity, [Bbuf, Bst], [Bbuf], scale=st[0:nt, so + 32:so + 33],
              bias=st[0:nt, so + 33:so + 34])
        p.tt(LN_ENG, buf_ap[0:nt, :], buf_ap[0:nt, :], vrow_sb[0:nt, og:og + 1024], ALU.mult, [Bbuf] + CST, [Bbuf])
        p.tt(bias_eng or LN_ENG, buf_ap[0:nt, :], buf_ap[0:nt, :], vrow_sb[0:nt, ob_:ob_ + 1024], ALU.add, [Bbuf] + CST, [Bbuf])

    def outproj_ln1(s, nt, res):
        for dh in range(2):
            hv = h1[0:nt, s, dh * 512:(dh + 1) * 512]
            bA, BA = res.psum()
            for c in range(4):
                p.mm(bA[0:nt, :], mixT[:, c, s * nt:(s + 1) * nt], w_out_sb[:, c, dh * 512:(dh + 1) * 512], c == 0, c == 3,
                     [B_mixT[c][s], B_wout[0]], [BA])
            p.stt(hv, bA[0:nt, :], rstd_sb[0:nt, s:s + 1], vrow_sb[0:nt, OFF_BOUT + dh * 512:OFF_BOUT + (dh + 1) * 512],
                  ALU.mult, ALU.add, [BA, B_rstd] + CST, [B_h1[s]])
            bB, BB = res.psum()
            for c in range(4, 8):
                p.mm(bB[0:nt, :], mixT[:, c, s * nt:(s + 1) * nt], w_out_sb[:, c, dh * 512:(dh + 1) * 512], c == 4, c == 7,
                     [B_mixT[c][s], B_wout[1]], [BB])
            p.stt(hv, bB[0:nt, :], rstd_sb[0:nt, 4 + s:5 + s], hv, ALU.mult, ALU.add, [BB, B_rstda[s], B_h1[s]], [B_h1[s]])
            p.stt(hv, xtok[0:nt, s, dh * 512:(dh + 1) * 512], ALPHA, hv, ALU.mult, ALU.add, [B_xtok[s], B_h1[s]], [B_h1[s]])
        layernorm(h1[:, s, :], B_h1[s], nt, OFF_L1G, OFF_L1B, res)
        for g in range(2):
            bank, Bb = res.psum()
            for j in range(4):
                kc = g * 4 + j
                p.tr(bank[:, j * 128:j * 128 + nt], h1[0:nt, s, kc * 128:(kc + 1) * 128], ident_f[0:nt, 0:nt],
                     [B_h1[s]] + CST, [Bb])
            src = bank[:].rearrange("p (a b) -> p a b", b=128)[:, :, 0:nt]
            p.cp("act", h1T[:, g * 4:(g + 1) * 4, s * nt:(s + 1) * nt], src, [Bb], [B_h1T[s]])

    wgu_n = [0]
    wdn_n = [0]

    def gu_load(f):
        sl = wgu_n[0] % 3
        wgu_n[0] += 1
        p.dma("pool", wgu[:, sl, 0, :, :], wg_b[:, f * 128:(f + 1) * 128].rearrange("(k p) n -> p k n", p=128),
              reads=[B_wgb], writes=[B_wg[sl]], chan="wg%d" % sl)
        p.dma("pool", wgu[:, sl, 1, :, :], wu_b[:, f * 128:(f + 1) * 128].rearrange("(k p) n -> p k n", p=128),
              reads=[B_wub], writes=[B_wu[sl]], chan="wu%d" % sl)

    def dn_load(i):
        dh, fp = i // (NF // 2), i % (NF // 2)
        sl = wdn_n[0] % 3
        wdn_n[0] += 1
        p.dma("pool", wdn[:, sl, :, :],
              wd_b[fp * 256:(fp + 1) * 256, dh * 512:(dh + 1) * 512].rearrange("(f p) n -> p f n", p=128),
              reads=[B_wdb], writes=[B_wdn[sl]], chan="wd%d" % sl)

    def ffn_weight_loads():
        base = (wgu_n[0], wdn_n[0])
        for f in range(3):
            gu_load(f)
        for i in range(3):
            dn_load(i)
        return base

    def ffn(ncol, nsub, nt, base, store, prefetch=None, head=None, tail=False):
        gbase, dbase = base
        h1Tr = B_h1T[0:nsub]
        for f in range(NF):
            sl = (gbase + f) % 3
            bG, BG = psum6()
            for k in range(8):
                p.mm(bG[:, 0:ncol], wgu[:, sl, 0, k, :], h1T[:, k, 0:ncol], k == 0, k == 7, [B_wg[sl]] + h1Tr, [BG])
            bU, BU = psum6()
            for k in range(8):
                p.mm(bU[:, 0:ncol], wgu[:, sl, 1, k, :], h1T[:, k, 0:ncol], k == 0, k == 7, [B_wu[sl]] + h1Tr, [BU])
            if f + 3 < NF:
                gu_load(f + 3)
            sg_, Bsg = gettmp()
            p.act(sg_[:, 0:ncol], bG[:, 0:ncol], AF.Tanh, [BG], [Bsg], scale=0.5)
            p.stt(sg_[:, 0:ncol], sg_[:, 0:ncol], 1.0, bG[:, 0:ncol], ALU.add, ALU.mult, [Bsg, BG], [Bsg])
            p.stt(hidT[:, f, 0:ncol], sg_[:, 0:ncol], 0.5, bU[:, 0:ncol], ALU.mult, ALU.mult, [Bsg, BU], [B_hid[f]])
        if prefetch is not None:
            prefetch()
        i = 0
        for dh in range(2):
            accs = [psum() for _ in range(nsub)]
            for fp in range(NF // 2):
                sl = (dbase + i) % 3
                for f2 in range(2):
                    f = 2 * fp + f2
                    for s in range(nsub):
                        p.mm(accs[s][0][0:nt, :], hidT[:, f, s * nt:(s + 1) * nt], wdn[:, sl, f2, :], f == 0, f == NF - 1,
                             [B_hid[f], B_wdn[sl]], [accs[s][1]])
                if i + 3 < NF:
                    dn_load(i + 3)
                i += 1
            for s in range(nsub):
                hv = h1[0:nt, s, dh * 512:(dh + 1) * 512]
                p.stt(hv, hv, ALPHA, accs[s][0][0:nt, :], ALU.mult, ALU.add, [B_h1[s], accs[s][1]], [B_h1[s]])
        def ln2_chain(s):
            layernorm(h1[:, s, :], B_h1[s], nt, OFF_L2G, OFF_L2B, CH4[s], bias_eng="dve" if tail else None)
            store(s)

        p.merge([p.record(lambda s=s: ln2_chain(s)) for s in range(nsub)] + ([p.record(head)] if head is not None else []))

    def fm_to_tm(src_fn, ncols, dst_dram, chan):
        bank, Bb = psum()
        for c in range(4):
            ap, rds = src_fn(c)
            p.tr(bank[0:ncols, c * 128:(c + 1) * 128], ap, ident_f[:, :], rds + CST, [Bb])
        o_, Bo = gettmp()
        p.cp("dve", o_[0:ncols, 0:512], bank[0:ncols, :], [Bb], [Bo])
        p.dma("sp", dst_dram, o_[0:ncols, 0:512], reads=[Bo], chan=chan)

    def carry_kv():
        p.cp("act", kT[:, :, 0:128], kT[:, :, 512:640], [B_kT[4]], [B_kT[0]])
        if stage >= 13:
            p.cp("act", vaug[:, 0, :, :], vaug[:, 4, :, :], [B_vaug[4]], [B_vaug[0]])

    xT_alt = h1[:, 0:2, :].rearrange("p a b -> p (a b)").bitcast(BF16).rearrange("p (k n) -> p k n", k=8)
    XS_ALT = (xT_alt, [[B_h1[0], B_h1[1]]] * 4)
    XS_MAIN = (xT, [[b_] for b_ in B_xT])
    pre_bank = [0]

    def pre_psum():
        i = 4 + pre_bank[0] % 2
        pre_bank[0] += 1
        return psf[i], B_psf[i]

    def prefix_pass(n_pre):
        xs = [XS_MAIN, XS_ALT]
        t0 = 8 - n_pre

        def ldx(t, last):
            load_x(lambda s: xpre[t * 512 + s * 128:t * 512 + (s + 1) * 128, :], 4, 128,
                   (rope_sb[:, 3, :], ropex_d) if last else None)

        ldx(t0, n_pre == 1)
        transposes_x(4, 128, dst_xs=xs[0])
        for i in range(n_pre):
            last = i == n_pre - 1
            XS[0], XS[1] = xs[i % 2]
            chains = [p.record(lambda c=c: lru_chunk(c, 512, 4, 128, False, False, CHP[c])) for c in range(4)]
            if not last:
                ldx(t0 + i + 1, i + 1 == n_pre - 1)
                chains.append(p.record(lambda: transposes_x(4, 128, dst_xs=xs[(i + 1) % 2], bank_fn=pre_psum)))
            if last:
                ch_pq = ChainRes(0, banks=[4], tmps=[(xtok[:, 0, :], B_xtok[0]), (xtok[:, 1, :], B_xtok[1])], psb=[0, 1])
                chains.append(p.record(lambda: qkv_sub(3, 128, 4, 512, ch_pq, q=False)))
            p.merge(chains)
            if last:
                carry_kv()
        XS[0], XS[1] = XS_MAIN

    def rstd_lru(ssq_bank, Bssq, nsub, nt):
        p.mm(ssq_bank[0:nt, 0:8], zero_f[:, 0:nt], ones_f[:, 0:8], False, True, CST, [Bssq])
        v = ssq_bank[0:nt, 0:2 * nsub].rearrange("p (s two) -> p s two", two=2)[:, :, 0]
        p.ts("dve", st[0:nt, 160:160 + nsub], v, 1.0 / 512, EPS_RMS, ALU.mult, ALU.add, [Bssq], [B_st])
        p.act(st[0:nt, 164:164 + nsub], st[0:nt, 160:160 + nsub], AF.Sqrt, [B_st], [B_st])
        p.op("dve", lambda e: e.reciprocal(rstd_sb[0:nt, 0:nsub], st[0:nt, 164:164 + nsub]), [B_st], [B_rstd])

    def load_main_x(t):
        load_x(lambda s: xmain[t * 512 + s * 128:t * 512 + (s + 1) * 128, :], 4, 128, (rope_sb[:], ropem_d[t]))

    def main_tile(t, last, preloaded, head_done=False):
        p.fence(ARENA_B)
        base = ffn_weight_loads()
        if not preloaded:
            load_main_x(t)
        if not head_done:
            transposes_x(4, 128)
        ssq_bank, Bssq = psf[5], B_psf[5]
        p.mm(ssq_bank[0:128, 0:8], zero_f[:, 0:128], ones_f[:, 0:8], True, False, CST, [Bssq])
        ckp("m:ssqzero")
        def lru_super(cs, res):
            for c in cs:
                slot = lru_chunk(c, 512, 4, 128, False, True, res)
                lru_mix(c, slot, 512, 4, 128, ssq_bank, Bssq, res)

        def qkv_super():
            for s in range(4):
                lastsub = last and s == 3

                def v32out(ap, Bt):
                    p.dma("sp", vp, ap, reads=[Bt], chan="vp")

                qkv_sub(s, 128, s + 1, 128 + s * 128, CH_Q, q=True, want_k32=lastsub, want_v32=v32out if lastsub else None)
                if lastsub:
                    p.dma("sp", kp, k32[:, :], reads=[B_k32], chan="kp")

        p.merge([p.record(lambda: lru_super((0, 2), CH_LA)), p.record(lambda: lru_super((1, 3), CH_LB)),
                 p.record(qkv_super)])
        rstd_lru(ssq_bank, Bssq, 4, 128)
        p.fence(B_tq)
        ckp("m:rstd_lru")
        def attn_chain(s, i):
            kbs = [
                (128, (lambda j, hf: kT[:, 2 * j + hf, s * 128:s * 128 + 128]),
                 (lambda j: vaug[:, s, j, :]), maskp_sb[:, 0, :], [B_kT[s], B_vaug[s]]),
                (128, (lambda j, hf: kT[:, 2 * j + hf, 128 + s * 128:256 + s * 128]),
                 (lambda j: vaug[:, s + 1, j, :]), maskp_sb[:, 1, :], [B_kT[s + 1], B_vaug[s + 1]]),
            ]
            attention_j(s, kbs, CH[i])

        for s0 in (0, 2):
            p.merge([p.record(lambda s=s0 + i, i=i: attn_chain(s, i)) for i in range(2)])
        carry_kv()
        p.merge([p.record(lambda s=s: outproj_ln1(s, 128, CH4[s])) for s in range(4)], skew=True)
        if last:
            fm_to_tm(lambda c: (hist[:, c, 0:3], [B_hist[c]]), 3, convp, "convp")
            fm_to_tm(lambda c: (hcar[:, c, 0:1], [B_hcar[c]]), 1, lrup, "lrup")
        p.fence(ARENA_A)

        def store(s):
            p.dma("sp", yp[t * 512 + s * 128:t * 512 + (s + 1) * 128, :], h1[:, s, :], reads=[B_h1[s]], chan="y%d" % s)

        ffn(512, 4, 128, base, store, prefetch=None if last else (lambda: load_main_x(t + 1)),
            head=None if last else (lambda: transposes_x(4, 128, extra_w=B_hid[0:8])), tail=last)
        ckp("m:ffn")

    def apply_flag():
        for c in range(4):
            p.ts("dve", hcar[:, c, 0:1], hcar[:, c, 0:1], flag_sb[:, 0:1], None, ALU.mult, None, [B_hcar[c]] + CST, [B_hcar[c]])
            p.ts("dve", hist[:, c, 0:3], hist[:, c, 0:3], flag_sb[:, 0:1], None, ALU.mult, None, [B_hist[c]] + CST, [B_hist[c]])
        if stage >= 14:
            v0 = vaug[:, 0, :, :].rearrange("p j d -> p (j d)")
            p.tt("dve", v0, v0, flag_sb[:, 0:1].to_broadcast([128, 130]), ALU.mult, [B_vaug[0]] + CST, [B_vaug[0]])

    def sample_tile(prefetch, head=None):
        nt = NS
        p.fence(ARENA_B)
        base = ffn_weight_loads()
        load_x(lambda s: xs, 1, nt, (rope_sb[0:nt, 0, :], ropes_d))
        sc_t, Bsc = gettmp()
        p.dma("sp", sc_t[0:48, 0:512], sconv, writes=[Bsc], chan="sconv")
        p.dma("sp", sc_t[0:16, 512:1024], slru, writes=[Bsc], chan="slru")
        ckst = h1[:, 0:2, :].rearrange("p a b -> p (a b)").rearrange("p (b c) -> p b c", c=128)
        cvst = h1[:, 2:4, :].rearrange("p a b -> p (a b)").rearrange("p (b c) -> p b c", c=128)
        p.dma("sp", ckst, ck.rearrange("b w c -> w b c"), writes=[B_h1[0], B_h1[1]], chan="ck")
        p.dma("sp", cvst, cv.rearrange("b w c -> w b c"), writes=[B_h1[2], B_h1[3]], chan="cv")
        p.dma("sp", ks[:, 0:124, :], ck[:, 4:128, :], chan="ksc")
        p.dma("sp", vs[:, 0:124, :], cv[:, 4:128, :], chan="vsc")
        transposes_x(1, nt)
        for c in range(4):
            bank, Bb = psum()
            p.tr(bank[:, 0:48], sc_t[0:48, c * 128:(c + 1) * 128], ident_f[0:48, 0:48], [Bsc] + CST, [Bb])
            p.tr(bank[:, 64:80], sc_t[0:16, 512 + c * 128:512 + (c + 1) * 128], ident_f[0:16, 0:16], [Bsc] + CST, [Bb])
            p.cp("dve", hist[:, c, 0:48], bank[:, 0:48], [Bb], [B_hist[c]])
            p.cp("dve", hcar[:, c, 0:16], bank[:, 64:80], [Bb], [B_hcar[c]])
        kcz = h1T[:, 0:2, :].rearrange("p a b -> p (a b)").rearrange("p (r x) -> p r x", r=2)
        kTc = hidT[:, 20:22, :].rearrange("p a b -> p (a b)").rearrange("p (r j w) -> p r j w", r=2, w=128)
        p.memset("dve", kcz, 0.0, [B_kcd])
        vaugc = xtok[:, 1:3, :].rearrange("p a b -> p (a b)").bitcast(BF16)[:, 0:NB * 2 * 65].rearrange(
            "p (b j d) -> p b j d", j=2, d=65)
        B_vc = [B_xtok[1], B_xtok[2]]
        p.cp("act", vaugc[:, :, :, 0:64], cvst.rearrange("p b (j d) -> p b j d", d=64), [B_h1[2], B_h1[3]], B_vc)
        p.memset("dve", vaugc[:, :, :, 64:65], 1.0, B_vc)
        ssq_bank, Bssq = psf[5], B_psf[5]
        p.mm(ssq_bank[0:nt, 0:8], zero_f[:, 0:nt], ones_f[:, 0:8], True, False, CST, [Bssq])
        def s_chain(c):
            slot = lru_chunk(c, nt, 1, nt, True, True, CH[c % 2])
            p.cp("dve", hcar[:, c, 0:16], hh[:, slot, 48:64], [B_hh[slot]], [B_hcar[c]])
            lru_mix(c, slot, nt, 1, nt, ssq_bank, Bssq, CH[c % 2], last_chain=(c == 3))

        for c0 in (0, 2):
            p.merge([p.record(lambda c=c0 + i: s_chain(c)) for i in range(2)])
        rstd_lru(ssq_bank, Bssq, 1, nt)
        fm_to_tm(lambda c: (hist[:, c, 0:48], [B_hist[c]]), 48, convs, "convs")
        fm_to_tm(lambda c: (hcar[:, c, 0:16], [B_hcar[c]]), 16, lrus, "lrus")

        def v32out(ap, Bt):
            for t in range(4):
                p.dma("sp", vs[:, 124 + t, :], ap[t * 16:(t + 1) * 16, :], reads=[Bt], chan="vs%d" % t)

        qkv_sub(0, nt, 1, 128, CH_S, q=True, want_k32=True, want_v32=v32out)
        for t in range(4):
            p.dma("sp", ks[:, 124 + t, :], k32[t * 16:(t + 1) * 16, :], reads=[B_k32], chan="ks%d" % t)
        B_kcd2 = [B_kcd, p.buf("kcd1")]
        ob = [(psf[4], B_psf[4]), (psf[5], B_psf[5])]
        for j in range(2):
            p.mm(ob[j][0][0:nt, 0:260], zero_b[:, 0:nt], zero_b[:, 0:260], True, False, CST, [ob[j][1]])

        def block(ch, nk, kTfn, vfn, mask, rds):
            bank, Bb = psf[ch], B_psf[ch]
            pT = pT_all[:, 2 * ch, :].rearrange("p (h q) -> p h q", q=nt)
            BpT = B_pT[ch]
            for h in range(8):
                p.mm(bank[0:nk, h * nt:(h + 1) * nt], kTfn(h // 4, h % 2), qT[:, h // 2, 0:nt], True, True, rds + [B_qT[0]], [Bb])
            pv = pT[0:nk, :, :]
            p.act(pv, bank[0:nk, 0:8 * nt].rearrange("p (g q) -> p g q", q=nt), AF.Exp, [Bb], [BpT], scale=SCALE)
            p.tt("dve", pv, pv, mask.unsqueeze(1).broadcast_to([nk, 8, nt]), ALU.mult, [BpT] + CST, [BpT])
            for h in range(8):
                j = h // 4
                p.mm(ob[j][0][0:nt, (h % 4) * 65:(h % 4 + 1) * 65], pT[0:nk, h, :], vfn(j), False, False,
                     rds + [BpT], [ob[j][1]])

        def chain(ch):
            for b in range(ch, NB, 2):
                for hf_ in range(2):
                    p.cp("act", kcz[:, ch, :].rearrange("p (j x) -> p j x", j=2)[:, :, hf_ * 192:hf_ * 192 + 64],
                         ckst[:, b, :].rearrange("p (j d) -> p j d", d=64), [B_h1[0], B_h1[1]], [B_kcd2[ch]])
                pb, Bpb = psbs[ch][:, 0:512], B_psb[ch]
                for j in range(4):
                    p.tr(pb[:, j * 128:(j + 1) * 128], kcz[:, ch, j * 128:(j + 1) * 128], ident_b[:, :], [B_kcd2[ch]] + CST, [Bpb])
                p.cp("dve", kTc[:, ch, :, :], pb[:, 0:512].rearrange("p (j w) -> p j w", w=128), [Bpb], [B_kTc[ch]])
                block(ch, 128, (lambda j, hf: kTc[:, ch, 2 * j + hf, :]), (lambda j, b=b: vaugc[:, b, j, :]),
                      masks_sb[:, b, :], [B_kTc[ch]] + B_vc)
            if ch == 1:
                block(ch, nt, (lambda j, hf: kT[:, 2 * j + hf, 128:128 + nt]), (lambda j: vaug[0:nt, 1, j, :]),
                      masks_sb[0:nt, 16, :], [B_kT[1], B_vaug[1]])

        p.merge([p.record(lambda: chain(0)), p.record(lambda: chain(1))])
        for j in range(2):
            p.mm(ob[j][0][0:nt, 0:260], zero_b[:, 0:nt], zero_b[:, 0:260], False, True, CST, [ob[j][1]])
        attn_tail(0, nt, ob, CH[0])
        p.fence(B_kcd2)
        outproj_ln1(0, nt, CH[0])
        p.fence(ARENA_A + B_vc)

        def store(s):
            p.dma("sp", ys, h1[0:nt, 0, :], reads=[B_h1[0]], chan="y0")

        ffn(nt, 1, nt, base, store, prefetch=prefetch, head=head)

    def program():
      if n_pre > 0:
          prefix_pass(n_pre)
      if n_pre > 0:
          p.fence(PREFIX_ALIAS + B_gg)
      preloaded = False
      if do_sample:
          keep_k = mixT[:, 0:4, 128:256]
          keep_v = mixT[:, 4:6, 128:193]
          keep_s = st[:, 168:192]
          Bkk = p.buf("keepk")
          p.cp("act", keep_k, kT[:, :, 0:128], [B_kT[0]], [Bkk])
          p.cp("act", keep_v, vaug[:, 0, :, :], [B_vaug[0]], [Bkk])
          p.cp("dve", keep_s[:, 0:12].rearrange("p (c j) -> p c j", j=3), hist[:, :, 0:3], B_hist, [Bkk])
          p.cp("dve", keep_s[:, 16:20], hcar[:, :, 0], B_hcar, [Bkk])
          sample_tile((lambda: load_main_x(0)) if n_main > 0 else None,
                      head=(lambda: transposes_x(4, 128, extra_w=B_hid[0:8])) if n_main > 0 else None)
          preloaded = n_main > 0
          p.cp("dve", hist[:, :, 0:3], keep_s[:, 0:12].rearrange("p (c j) -> p c j", j=3), [Bkk], B_hist)
          p.cp("dve", hcar[:, :, 0], keep_s[:, 16:20], [Bkk], B_hcar)
          p.cp("act", kT[:, :, 0:128], keep_k, [Bkk], [B_kT[0]])
          p.cp("act", vaug[:, 0, :, :], keep_v, [Bkk], [B_vaug[0]])
      apply_flag()
      for t in range(n_main):
          main_tile(t, t == n_main - 1, preloaded, head_done=(t > 0 or (do_sample and n_main > 0)))
          preloaded = t < n_main - 1


    try:
        program()
    except Cut:
        pass
    if dbg:
        print("checkpoints:", len(cks))
    p.emit()
    es.close()
    return nc


_CACHE = {}


def _consts():
    if "c" in _CACHE:
        return _CACHE["c"]
    ident = np.eye(128, dtype=np.float32)
    sp = np.arange(128)[:, None]
    qp = np.arange(128)[None, :]
    maskp = np.stack([(sp >= qp), (sp <= qp)], axis=1).astype(np.float32)
    masks = np.zeros((128, 17, 64), np.float32)
    tq = np.arange(64) // 16
    bq = np.arange(64) % 16
    w = np.arange(128)
    for b in range(16):
        masks[:, b, :] = ((bq[None, :] == b) & (w[:, None] >= tq[None, :])).astype(np.float32)
    tk = np.arange(64) // 16
    bk = np.arange(64) % 16
    masks[0:64, 16, :] = ((bk[:, None] == bq[None, :]) & (tk[:, None] <= tq[None, :])).astype(np.float32)
    inv = 10000.0 ** (-np.arange(32, dtype=np.float64) / 32.0)

    def table(pos):
        ang = pos.astype(np.float64)[:, None] * inv[None, :]
        c = np.cos(ang).astype(np.float32)
        s = np.sin(ang).astype(np.float32)
        return np.concatenate([c, s, -s], axis=1).astype(np.float32)

    _CACHE["c"] = (ident, maskp, masks, table)
    return _CACHE["c"]


def _get_nc():
    if "nc" not in _CACHE:
        _CACHE["nc"] = build_program()
    return _CACHE["nc"]


def _prep_inputs(inp):
    ident, maskp, masks, table = _consts()
    f = lambda a: np.ascontiguousarray(np.asarray(a, dtype=np.float32))
    x_prompt = f(inp["x_prompt"])
    x_sample = f(inp["x_sample"])
    ck = f(inp["cache_k_win"])[0].reshape(128, 128, 128)
    cv = f(inp["cache_v_win"])[0].reshape(128, 128, 128)
    sconv = f(inp["state_conv"])[0]
    slru = f(inp["state_lru"])[0]
    w_in = f(inp["w_in"])[0]
    b_in = f(inp["b_in"])[0]
    conv_w = f(inp["conv_w"])[0]
    conv_b = f(inp["conv_b"])[0]
    w_a = f(inp["w_a"])[0]
    w_x = f(inp["w_x"])[0]
    b_a = f(inp["b_a"])[0].reshape(512)
    b_x = f(inp["b_x"])[0].reshape(512)
    lam = f(inp["lru_lambda"])[0]
    g_lru = f(inp["g_lru"])[0]

    def blockdiag(wm):
        o = np.zeros((128, 4, 128), np.float32)
        for c in range(4):
            for u in range(2):
                o[u * 64:(u + 1) * 64, c, u * 64:(u + 1) * 64] = wm[2 * c + u]
        return o

    fm = lambda v: np.ascontiguousarray(v.reshape(4, 128).T)
    bfm = np.zeros((128, NBF), np.float32)
    bfm[:, C_BXR:C_BXR + 4] = fm(b_in[0:512])
    bfm[:, C_BGATE:C_BGATE + 4] = fm(b_in[512:1024])
    for j in range(4):
        bfm[:, C_CONVW + 4 * j:C_CONVW + 4 * j + 4] = fm(conv_w[j])
    bfm[:, C_CONVB:C_CONVB + 4] = fm(conv_b)
    bfm[:, C_BA:C_BA + 4] = fm(b_a)
    bfm[:, C_BX:C_BX + 4] = fm(b_x)
    bfm[:, C_LAM:C_LAM + 4] = fm(lam)
    bfm[:, C_GLRU:C_GLRU + 4] = fm(g_lru)
    vrow = np.concatenate([b_in[1024:1792], f(inp["g_attn"])[0], f(inp["b_out"])[0], f(inp["ln1_g"])[0],
                           f(inp["ln1_b"])[0], f(inp["ln2_g"])[0], f(inp["ln2_b"])[0], f(inp["sinks"])[0]])[None, :]
    vrow = np.ascontiguousarray(vrow.astype(np.float32))
    shared = dict(w_in=w_in, w_out=f(inp["w_out"])[0], w_gate=f(inp["w_gate"])[0], w_up=f(inp["w_up"])[0],
                  w_down=f(inp["w_down"])[0], wab=blockdiag(w_a), wxb=blockdiag(w_x), bfm=bfm, vrow=vrow,
                  maskp=maskp, masks=masks, ident=ident)
    ropes = table(PAST + np.arange(64) // 16)
    in_maps = []
    for c in range(NCORES):
        b, hf = c // 2, c % 2
        xm = x_prompt[b, hf * HALF:(hf + 1) * HALF]
        xp = x_prompt[b, 0:HALF]
        pos0 = hf * HALF
        ropem = table(pos0 + np.arange(HALF)).reshape(8, 4, 128, 96).transpose(0, 2, 1, 3)
        ropex = table(np.maximum(pos0 - 128 + np.arange(128), 0))
        bs = slice(c * NB, (c + 1) * NB)
        m = dict(shared)
        m.update(
            xpre=np.ascontiguousarray(xp), xmain=np.ascontiguousarray(xm),
            xs=np.ascontiguousarray(x_sample[bs].transpose(1, 0, 2).reshape(NS, D)),
            ck=np.ascontiguousarray(ck[bs]), cv=np.ascontiguousarray(cv[bs]),
            sconv=np.ascontiguousarray(sconv[bs].transpose(1, 0, 2).reshape(48, LRUW)),
            slru=np.ascontiguousarray(slru[bs]),
            ropem=np.ascontiguousarray(ropem), ropex=np.ascontiguousarray(ropex), ropes=ropes,
            flag=np.full((128, 1), float(hf), np.float32),
        )
        in_maps.append(m)
    return in_maps


def _assemble(res):
    y_prompt = np.empty((4, SEQ, D), np.float32)
    y_sample = np.empty((128, 4, D), np.float32)
    conv_p = np.empty((1, 4, 3, LRUW), np.float32)
    lru_p = np.empty((1, 4, LRUW), np.float32)
    k_p = np.empty((1, 4, 128, 2, 64), np.float32)
    v_p = np.empty((1, 4, 128, 2, 64), np.float32)
    conv_s = np.empty((1, 128, 3, LRUW), np.float32)
    lru_s = np.empty((1, 128, LRUW), np.float32)
    k_s = np.empty((1, 128, 128, 2, 64), np.float32)
    v_s = np.empty((1, 128, 128, 2, 64), np.float32)
    for c in range(NCORES):
        r = res[c]
        b, hf = c // 2, c % 2
        y_prompt[b, hf * HALF:(hf + 1) * HALF] = r["yp"]
        bs = slice(c * NB, (c + 1) * NB)
        y_sample[bs] = r["ys"].reshape(4, NB, D).transpose(1, 0, 2)
        conv_s[0, bs] = r["convs"].reshape(3, NB, LRUW).transpose(1, 0, 2)
        lru_s[0, bs] = r["lrus"]
        k_s[0, bs] = r["ks"].reshape(NB, 128, 2, 64)
        v_s[0, bs] = r["vs"].reshape(NB, 128, 2, 64)
        if hf == 1:
            conv_p[0, b] = r["convp"]
            lru_p[0, b] = r["lrup"][0]
            k_p[0, b] = r["kp"].reshape(128, 2, 64)
            v_p[0, b] = r["vp"].reshape(128, 2, 64)
    return (y_prompt, y_sample, conv_p, lru_p, k_p, v_p, conv_s, lru_s, k_s, v_s)


def kernel(**inputs):
    nc = _get_nc()
    in_maps = _prep_inputs(inputs)
    res = run_bass_kernel_spmd(nc, in_maps, core_ids=list(range(NCORES)))
    return _assemble(res.results)
```
